# Optimizing a Trainium2 kernel written in Bass

```python
import math
import jax, jax.numpy as jnp
from jax import lax
import numpy as np

D_MODEL = 1024
BATCH = 8
SEQ = 4096
DEPTH = 2
DEC_BATCH = 32
DEC_SEQ = 8
PAST_LEN = 16384
PAGE_SIZE = 128

HEAD_DIM = 64
N_HEAD_SLOTS = 8
DIL_GROUPS = ((128, 1), (512, 4), (2048, 16))
N_DIL = len(DIL_GROUPS)
ATTN_WIDTH = N_HEAD_SLOTS * HEAD_DIM
QKV_WIDTH = 3 * N_DIL * ATTN_WIDTH
ROPE_THETA = 10000.0
BLOCK_Q = 128
SSM_WIDTH = D_MODEL
SSM_GROUP = 16
SSM_GROUPS = SSM_WIDTH // SSM_GROUP
SSM_STATE = 64
SCAN_CHUNK = 128
D_FF = -(-8 * D_MODEL // (3 * 256)) * 256
RMS_EPS = 1e-6
N_ATTN_LAYERS = (DEPTH + 1) // 2
N_SSM_LAYERS = DEPTH // 2

kernel_name = 'dilated_attn_s5_hybrid_step'


def rmsnorm(x, g):
    xf = x.astype(jnp.float32)
    y = xf * lax.rsqrt(jnp.mean(xf * xf, axis=-1, keepdims=True) + RMS_EPS)
    return (y * g.astype(jnp.float32)).astype(x.dtype)


def rope(x, pos):
    half = HEAD_DIM // 2
    inv = ROPE_THETA ** (-jnp.arange(half, dtype=jnp.float32) / half)
    ang = pos.astype(jnp.float32)[:, None] * inv[None, :]
    cos = jnp.cos(ang)[None, :, None, None, :]
    sin = jnp.sin(ang)[None, :, None, None, :]
    xf = x.astype(jnp.float32)
    x1, x2 = xf[..., :half], xf[..., half:]
    return jnp.concatenate([x1 * cos - x2 * sin, x2 * cos + x1 * sin], axis=-1).astype(x.dtype)


def dilated_block(q_blk, kv_all, qi, dilation, n_keys):
    idx = qi[:, None] - dilation * jnp.arange(n_keys)[None, :]
    valid = idx >= 0
    kvg = jnp.take(kv_all, jnp.maximum(idx, 0), axis=1)
    kf = kvg[:, :, :, 0].astype(jnp.float32)
    vf = kvg[:, :, :, 1].astype(jnp.float32)
    s = jnp.einsum('bqhd,bqkhd->bqhk', q_blk.astype(jnp.float32), kf) * (HEAD_DIM ** -0.5)
    s = jnp.where(valid[None, :, None, :], s, -jnp.inf)
    lse = jax.nn.logsumexp(s, axis=-1)
    p = jnp.exp(s - lse[..., None])
    o = jnp.einsum('bqhk,bqkhd->bqhd', p, vf)
    return o, lse


def attention_layer(h, pos, bufs, w_qkv, q_gain, k_gain, w_o):
    B, T, _ = h.shape
    qkv = (h @ w_qkv).reshape(B, T, 3, N_DIL, N_HEAD_SLOTS, HEAD_DIM)
    q = rope(rmsnorm(qkv[:, :, 0], q_gain), pos)
    k = rope(rmsnorm(qkv[:, :, 1], k_gain), pos)
    v = qkv[:, :, 2]
    kv_alls, new_bufs, offs = [], [], []
    for g, (win, dil) in enumerate(DIL_GROUPS):
        kv_new = jnp.stack([k[:, :, g], v[:, :, g]], axis=2)
        kv_all = jnp.concatenate([bufs[g].astype(kv_new.dtype), kv_new], axis=1)
        kv_alls.append(kv_all)
        offs.append(bufs[g].shape[1])
        keep = min(win, kv_all.shape[1])
        new_bufs.append(kv_all[:, kv_all.shape[1] - keep:])
    qb = BLOCK_Q if T % BLOCK_Q == 0 else T
    n_blocks = T // qb

    def block(i):
        outs, lses = [], []
        for g, (win, dil) in enumerate(DIL_GROUPS):
            q_blk = lax.dynamic_slice_in_dim(q[:, :, g], i * qb, qb, axis=1)
            qi = offs[g] + i * qb + jnp.arange(qb)
            o, l = dilated_block(q_blk, kv_alls[g], qi, dil, win // dil + 1)
            outs.append(o)
            lses.append(l)
        wts = jax.nn.softmax(jnp.stack(lses, axis=0), axis=0)
        return jnp.einsum('gbqh,gbqhd->bqhd', wts, jnp.stack(outs, axis=0))

    o = lax.map(block, jnp.arange(n_blocks))
    o = jnp.moveaxis(o, 0, 1).reshape(B, T, ATTN_WIDTH).astype(h.dtype)
    return o @ w_o, new_bufs


def ssm_layer(h, state0, w_in, a_re, a_im, log_dt, b_re, b_im, c_re, c_im, d_skip, w_glu):
    B, T, _ = h.shape
    f32 = jnp.float32
    u = (h @ w_in).astype(f32)
    lam = lax.complex(a_re.astype(f32), a_im.astype(f32))
    dt = jnp.exp(log_dt.astype(f32))[:, None]
    abar = jnp.exp(lam * dt)
    bmat = lax.complex(b_re.astype(f32), b_im.astype(f32))
    bbar = ((abar - 1.0) / lam)[..., None] * bmat
    cmat = lax.complex(c_re.astype(f32), c_im.astype(f32))
    ch = SCAN_CHUNK if T % SCAN_CHUNK == 0 else T
    nc = T // ch
    uc = jnp.moveaxis(u.reshape(B, nc, ch, SSM_GROUPS, SSM_GROUP), 1, 0)

    def combine(e1, e2):
        a1, b1 = e1
        a2, b2 = e2
        return a1 * a2, a2 * b1 + b2

    def step(hc, uk):
        bu = jnp.einsum('gpc,btgc->btgp', bbar, uk.astype(jnp.complex64))
        a_el = jnp.broadcast_to(abar, bu.shape)
        a_cum, x0 = lax.associative_scan(combine, (a_el, bu), axis=1)
        xs = x0 + a_cum * hc[:, None]
        y = jnp.einsum('gcp,btgp->btgc', cmat, xs).real
        return xs[:, -1], y

    h_last, ys = lax.scan(step, state0, uc)
    y = jnp.moveaxis(ys, 0, 1).reshape(B, T, SSM_WIDTH) + d_skip.astype(f32) * u
    g = jax.nn.gelu(y).astype(h.dtype)
    z = g @ w_glu
    val, gate = jnp.split(z, 2, axis=-1)
    return val * jax.nn.sigmoid(gate), h_last


def swiglu(h, w_gate, w_up, w_down):
    return (jax.nn.silu(h @ w_gate) * (h @ w_up)) @ w_down


def trunk(x, pos, attn_bufs, ssm_states, p):
    new_attn, new_ssm = [], []
    for i in range(DEPTH):
        h = rmsnorm(x, p['norm_mix'][i])
        if i % 2 == 0:
            la = i // 2
            out, nb = attention_layer(h, pos, attn_bufs[la], p['w_qkv'][la], p['q_norm'][la],
                                      p['k_norm'][la], p['w_o'][la])
            new_attn.append(nb)
        else:
            lb = i // 2
            out, st = ssm_layer(h, ssm_states[lb], p['ssm_w_in'][lb], p['ssm_a_re'][lb],
                                p['ssm_a_im'][lb], p['ssm_log_dt'][lb], p['ssm_b_re'][lb],
                                p['ssm_b_im'][lb], p['ssm_c_re'][lb], p['ssm_c_im'][lb],
                                p['ssm_d'][lb], p['ssm_w_glu'][lb])
            new_ssm.append(st)
        x = x + out.astype(x.dtype)
        h2 = rmsnorm(x, p['norm_ffn'][i])
        x = x + swiglu(h2, p['ffn_w_gate'][i], p['ffn_w_up'][i], p['ffn_w_down'][i]).astype(x.dtype)
    return x, new_attn, new_ssm


def _stack_kv(kv_layers, g):
    return jnp.stack([layer[g] for layer in kv_layers], axis=0)


def _stack_state(states, dtype):
    return jnp.stack([jnp.stack([s.real, s.imag], axis=-1) for s in states], axis=0).astype(dtype)


def setup_inputs(seed: int = 0) -> dict:
    key = jax.random.key(seed)
    ks = jax.random.split(key, 26)
    f32 = jnp.float32

    def nrm(k, shape, scale):
        return jax.random.normal(k, shape, f32) * scale

    na, nb = N_ATTN_LAYERS, N_SSM_LAYERS

    def cache_shape(win):
        return (na, DEC_BATCH, min(win, PAST_LEN), 2, N_HEAD_SLOTS, HEAD_DIM)

    n_idx = jnp.arange(SSM_STATE, dtype=f32)
    G, P, C = SSM_GROUPS, SSM_STATE, SSM_GROUP
    return {
        'x_prompt': nrm(ks[0], (BATCH, SEQ, D_MODEL), 1.0),
        'x_sample': nrm(ks[1], (DEC_BATCH, DEC_SEQ, D_MODEL), 1.0),
        'cache_kv_w128': nrm(ks[2], cache_shape(DIL_GROUPS[0][0]), 1.0),
        'cache_kv_w512': nrm(ks[3], cache_shape(DIL_GROUPS[1][0]), 1.0),
        'cache_kv_w2048': nrm(ks[4], cache_shape(DIL_GROUPS[2][0]), 1.0),
        'state_ssm': nrm(ks[5], (nb, DEC_BATCH, G, P, 2), 1.0),
        'norm_mix': 1.0 + nrm(ks[6], (DEPTH, D_MODEL), 0.02),
        'norm_ffn': 1.0 + nrm(ks[7], (DEPTH, D_MODEL), 0.02),
        'w_qkv': nrm(ks[8], (na, D_MODEL, QKV_WIDTH), D_MODEL ** -0.5),
        'q_norm': 1.0 + nrm(ks[9], (na, HEAD_DIM), 0.02),
        'k_norm': 1.0 + nrm(ks[10], (na, HEAD_DIM), 0.02),
        'w_o': nrm(ks[11], (na, ATTN_WIDTH, D_MODEL), ATTN_WIDTH ** -0.5),
        'ssm_w_in': nrm(ks[12], (nb, D_MODEL, SSM_WIDTH), D_MODEL ** -0.5),
        'ssm_a_re': -0.5 + nrm(ks[13], (nb, G, P), 0.01),
        'ssm_a_im': math.pi * n_idx + nrm(ks[14], (nb, G, P), 0.01),
        'ssm_log_dt': jax.random.uniform(ks[15], (nb, G), f32, math.log(1e-3), math.log(1e-1)),
        'ssm_b_re': nrm(ks[16], (nb, G, P, C), (2 * C) ** -0.5),
        'ssm_b_im': nrm(ks[17], (nb, G, P, C), (2 * C) ** -0.5),
        'ssm_c_re': nrm(ks[18], (nb, G, C, P), (2 * P) ** -0.5),
        'ssm_c_im': nrm(ks[19], (nb, G, C, P), (2 * P) ** -0.5),
        'ssm_d': nrm(ks[20], (nb, SSM_WIDTH), 1.0),
        'ssm_w_glu': nrm(ks[21], (nb, SSM_WIDTH, 2 * D_MODEL), SSM_WIDTH ** -0.5),
        'ffn_w_gate': nrm(ks[22], (DEPTH, D_MODEL, D_FF), D_MODEL ** -0.5),
        'ffn_w_up': nrm(ks[23], (DEPTH, D_MODEL, D_FF), D_MODEL ** -0.5),
        'ffn_w_down': nrm(ks[24], (DEPTH, D_FF, D_MODEL), D_FF ** -0.5),
    }


def reference(x_prompt, x_sample, cache_kv_w128, cache_kv_w512, cache_kv_w2048, state_ssm,
              norm_mix, norm_ffn, w_qkv, q_norm, k_norm, w_o,
              ssm_w_in, ssm_a_re, ssm_a_im, ssm_log_dt, ssm_b_re, ssm_b_im, ssm_c_re, ssm_c_im,
              ssm_d, ssm_w_glu, ffn_w_gate, ffn_w_up, ffn_w_down):
    p = dict(norm_mix=norm_mix, norm_ffn=norm_ffn, w_qkv=w_qkv, q_norm=q_norm, k_norm=k_norm,
             w_o=w_o, ssm_w_in=ssm_w_in, ssm_a_re=ssm_a_re, ssm_a_im=ssm_a_im,
             ssm_log_dt=ssm_log_dt, ssm_b_re=ssm_b_re, ssm_b_im=ssm_b_im, ssm_c_re=ssm_c_re,
             ssm_c_im=ssm_c_im, ssm_d=ssm_d, ssm_w_glu=ssm_w_glu, ffn_w_gate=ffn_w_gate,
             ffn_w_up=ffn_w_up, ffn_w_down=ffn_w_down)
    bp = x_prompt.shape[0]
    empty = jnp.zeros((bp, 0, 2, N_HEAD_SLOTS, HEAD_DIM), x_prompt.dtype)
    prompt_bufs = [[empty for _ in range(N_DIL)] for _ in range(N_ATTN_LAYERS)]
    prompt_states = [jnp.zeros((bp, SSM_GROUPS, SSM_STATE), jnp.complex64) for _ in range(N_SSM_LAYERS)]
    pos_p = jnp.arange(x_prompt.shape[1])
    y_prompt, kv_p, st_p = trunk(x_prompt, pos_p, prompt_bufs, prompt_states, p)
    caches = (cache_kv_w128, cache_kv_w512, cache_kv_w2048)
    sample_bufs = [[c[l] for c in caches] for l in range(N_ATTN_LAYERS)]
    sample_states = [lax.complex(state_ssm[l, ..., 0].astype(jnp.float32),
                                 state_ssm[l, ..., 1].astype(jnp.float32)) for l in range(N_SSM_LAYERS)]
    pos_s = PAST_LEN + jnp.arange(x_sample.shape[1])
    y_sample, kv_s, st_s = trunk(x_sample, pos_s, sample_bufs, sample_states, p)
    return (y_prompt, y_sample,
            _stack_kv(kv_p, 0), _stack_kv(kv_p, 1), _stack_kv(kv_p, 2),
            _stack_state(st_p, x_prompt.dtype),
            _stack_kv(kv_s, 0), _stack_kv(kv_s, 1), _stack_kv(kv_s, 2),
            _stack_state(st_s, x_sample.dtype))
```

```python
import numpy as np
import concourse.bass as bass
import concourse.mybir as mybir
from concourse.bass_utils import run_bass_kernel_spmd

F32 = mybir.dt.float32
BF16 = mybir.dt.bfloat16
AF = mybir.ActivationFunctionType
ALU = mybir.AluOpType
AX = mybir.AxisListType


def M(name, *a, **k):
    return (name, a, k)


def _mkfn(spec):
    name, a, k = spec
    return lambda e: getattr(e, name)(*a, **k)


class _Op:
    __slots__ = ("eng", "fn", "reads", "writes", "dma", "signal", "count", "waits", "idx", "_deps", "is_barrier", "spec", "alld", "rawd", "phase", "t_end", "pos", "_dma_all", "pdma")

    def __init__(self, eng, fn, reads, writes, dma):
        self.eng = eng
        self.fn = _mkfn(fn) if isinstance(fn, tuple) else fn
        self.spec = fn if isinstance(fn, tuple) else None
        self.reads = tuple(reads)
        self.writes = tuple(writes)
        self.dma = dma
        self.signal = False
        self.is_barrier = False
        self.count = 0
        self.waits = []


class Prog:
    ENGS = ("pe", "act", "dve", "pool", "sp")

    def __init__(self, nc):
        self.nc = nc
        self.ops = []

    def add(self, eng, fn, reads=(), writes=(), dma=None):
        op = _Op(eng, fn, reads, writes, dma)
        op.idx = len(self.ops)
        self.ops.append(op)
        return op

    def pe(self, fn, reads=(), writes=()):
        return self.add("pe", fn, reads, writes)

    def act(self, fn, reads=(), writes=()):
        return self.add("act", fn, reads, writes)

    def dve(self, fn, reads=(), writes=()):
        return self.add("dve", fn, reads, writes)

    def pool(self, fn, reads=(), writes=()):
        return self.add("pool", fn, reads, writes)

    def dma(self, key, out, in_, reads=(), writes=(), eng="sp", **kw):
        return self.add(eng, M("dma_start", out=out, in_=in_, **kw), reads, writes, dma=key)

    def barrier(self, dummy):
        op = _Op("pool", M("memset", dummy, 0.0), (), (), None)
        op.idx = len(self.ops)
        op.is_barrier = True
        self.ops.append(op)
        return op

    @staticmethod
    def _cost(op):
        spec = op.spec
        if spec is None:
            return 0.2
        name, a, k = spec
        def fsz(ap):
            n = 1
            for d in ap.shape[1:]:
                n *= d
            return n
        if op.dma is not None:
            return {"sp": 0.06, "act": 0.06, "pool": 1.0}.get(op.eng, 0.1)
        if name in ("matmul", "transpose"):
            mv = k.get("rhs") if name == "matmul" else k.get("in_")
            n = fsz(mv) if name == "matmul" else 128
            f = 4.0 if (name == "matmul" and k.get("lhsT").dtype == F32) else 1.0
            return 0.02 + f * max(64, n) / 2000.0
        out = k.get("out") if "out" in k else (a[0] if a else None)
        n = fsz(out) if out is not None else 64
        if op.eng == "act":
            return 0.22 + n / 1200.0
        if op.eng == "pool":
            return 0.25 + n / 600.0
        return 0.16 + n / 960.0

    @staticmethod
    def _dma_time(op):
        name, a, k = op.spec
        out = k["out"]
        n = 1
        for d in out.shape:
            n *= d
        nbytes = n * (2 if out.dtype == BF16 else 4)
        return 2.0 + nbytes / 60000.0

    def finalize(self):
        import heapq
        ops = self.ops
        last_w, readers = {}, {}
        phase = 0
        for op in ops:
            op.phase = phase
            if op.is_barrier:
                phase += 1
                op.alld, op.rawd = set(), set()
                continue
            alld, rawd = set(), set()
            for r in op.reads:
                w = last_w.get(r)
                if w is not None:
                    alld.add(w)
                    rawd.add(w)
            for wk in op.writes:
                w = last_w.get(wk)
                if w is not None:
                    alld.add(w)
                for rd in readers.get(wk, ()):
                    alld.add(rd)
            alld.discard(op)
            rawd.discard(op)
            op.alld = set(d for d in alld if d.phase == op.phase)
            op.rawd = rawd
            for r in op.reads:
                readers.setdefault(r, []).append(op)
            for wk in op.writes:
                last_w[wk] = op
                readers[wk] = []
        nphase = phase + 1
        by_phase = [[] for _ in range(nphase)]
        barriers = {}
        for op in ops:
            if op.is_barrier:
                barriers[op.phase] = op
            else:
                by_phase[op.phase].append(op)
        order = {e: [] for e in self.ENGS}
        tnow = 0.0
        for ph in range(nphase):
            pops = by_phase[ph]
            nun = {}
            users = {}
            for op in pops:
                nun[op] = len(op.alld)
                for d in op.alld:
                    users.setdefault(d, []).append(op)
            efree = {e: tnow for e in self.ENGS}
            wait_h = {e: [] for e in self.ENGS}
            avail_h = {e: [] for e in self.ENGS}
            ready_t = {}
            for op in pops:
                if nun[op] == 0:
                    heapq.heappush(wait_h[op.eng], (tnow, op.idx, op))
            done = 0
            tend = tnow
            while done < len(pops):
                best = None
                for e in self.ENGS:
                    wh, ah = wait_h[e], avail_h[e]
                    while wh and wh[0][0] <= efree[e]:
                        t_, i_, o_ = heapq.heappop(wh)
                        heapq.heappush(ah, (i_, o_))
                    if ah:
                        cand = (efree[e], ah[0][0], e, 0)
                    elif wh:
                        cand = (wh[0][0], wh[0][1], e, 1)
                    else:
                        continue
                    if best is None or cand < best:
                        best = cand
                assert best is not None, "scheduler deadlock"
                st_, _, e, src = best
                if src == 0:
                    _, op = heapq.heappop(avail_h[e])
                else:
                    _, _, op = heapq.heappop(wait_h[e])
                dur = self._cost(op)
                fin = st_ + dur
                efree[e] = fin
                comp = fin + (self._dma_time(op) if op.dma is not None else 0.0)
                op.t_end = comp
                tend = max(tend, comp)
                order[e].append(op)
                done += 1
                for u in users.get(op, ()):
                    nun[u] -= 1
                    rt = max(ready_t.get(u, tnow), comp + (0.25 if (u.eng != e or op.dma is not None) else 0.0))
                    ready_t[u] = rt
                    if nun[u] == 0:
                        heapq.heappush(wait_h[u.eng], (rt, u.idx, u))
            tnow = tend + 1.0
            if ph in barriers:
                order["pool"].append(barriers[ph])
        self.order = order
        self.est_us = tnow
        for e in self.ENGS:
            for i, op in enumerate(order[e]):
                op.pos = i
        seq = []
        for e in self.ENGS:
            seq.extend(order[e])
        last_eng = {}
        last_dma = {}
        cur_barrier = {}
        for op in ops:
            op._deps = []
        for ph in range(nphase):
            pass
        eng_last_by_phase = {}
        for e in self.ENGS:
            for op in order[e]:
                if not op.is_barrier and op.dma is None:
                    eng_last_by_phase[(e, op.phase)] = op
        per_phase_idx = {}
        per_phase_eng = {}
        for e in self.ENGS:
            for op in order[e]:
                if op.dma is not None:
                    d = per_phase_idx.setdefault(op.phase, {})
                    if op.dma not in d:
                        d[op.dma] = len(d)
                    op.pdma = d[op.dma]
                    assert per_phase_eng.setdefault((op.phase, op.dma), e) == e, "dma key %s used from two queues" % (op.dma,)
        for op in ops:
            if op.is_barrier:
                deps = []
                for e in self.ENGS:
                    for ph in range(op.phase, -1, -1):
                        if (e, ph) in eng_last_by_phase:
                            deps.append(eng_last_by_phase[(e, ph)])
                            break
                op._deps = deps
                op._dma_all = True
            else:
                keep = []
                newest = {}
                for d in op.alld:
                    if d.dma is not None or op.dma is not None:
                        keep.append(d)
                        continue
                    if d.eng == op.eng:
                        if op.eng == "pe" or d not in op.rawd:
                            continue
                    if d.eng not in newest or newest[d.eng].pos < d.pos:
                        newest[d.eng] = d
                keep.extend(newest.values())
                if op.phase > 0:
                    keep.append(barriers[op.phase - 1])
                op._deps = keep
                op._dma_all = False
        for op in ops:
            for d in op._deps:
                if d.dma is None:
                    d.signal = True
        eng_cnt = {e: 0 for e in self.ENGS}
        dma_cnt = {}
        dma_cnt_at_barrier = {}
        for e in self.ENGS:
            for op in order[e]:
                if op.dma is None and op.signal:
                    eng_cnt[e] += 1
                    op.count = eng_cnt[e]
        for ph in range(nphase):
            for e in self.ENGS:
                for op in order[e]:
                    if op.dma is not None and op.phase == ph:
                        dma_cnt[op.pdma] = dma_cnt.get(op.pdma, 0) + 16
                        op.count = dma_cnt[op.pdma]
            dma_cnt_at_barrier[ph] = dict(dma_cnt)
        waited = {e: {} for e in self.ENGS}
        for e in self.ENGS:
            wl = waited[e]
            for op in order[e]:
                need = {}
                for d in op._deps:
                    k = ("dma", d.pdma) if d.dma is not None else ("eng", d.eng)
                    need[k] = max(need.get(k, 0), d.count)
                if op._dma_all:
                    for k, v in dma_cnt_at_barrier[op.phase].items():
                        need[("dma", k)] = max(need.get(("dma", k), 0), v)
                for k, v in need.items():
                    if wl.get(k, 0) >= v:
                        continue
                    wl[k] = v
                    op.waits.append((k, v))
        self.dma_keys = list(dma_cnt.keys())
        self.dma_final = dma_cnt
        self.eng_final = eng_cnt

    def emit(self):
        nc = self.nc
        self.finalize()
        import contextlib
        with contextlib.ExitStack() as st:
            sems = {}
            for e in self.ENGS:
                sems[("eng", e)] = st.enter_context(nc.semaphore("s_" + e))
            for i, k in enumerate(self.dma_keys):
                sems[("dma", k)] = st.enter_context(nc.semaphore("d%d" % i))
            print("n_sems", len(sems), "n_ops", len(self.ops),
                  {e: sum(1 for o in self.ops if o.eng == e) for e in self.ENGS})
            block = st.enter_context(nc.Block())
            per = self.order
            print("scheduler estimate: %.1f us" % self.est_us)

            def run(eng, name):
                for op in per[name]:
                    for k, v in op.waits:
                        eng.wait_ge(sems[k], v)
                    ins = op.fn(eng)
                    if op.dma is not None:
                        ins.then_inc(sems[("dma", op.pdma)], 16)
                    elif op.signal:
                        ins.then_inc(sems[("eng", name)], 1)
                if name == "sp":
                    for k, v in self.dma_final.items():
                        eng.wait_ge(sems[("dma", k)], v)
                    for e, v in self.eng_final.items():
                        if v > 0:
                            eng.wait_ge(sems[("eng", e)], v)

            @block.tensor
            def _(e):
                run(e, "pe")

            @block.scalar
            def _(e):
                run(e, "act")

            @block.vector
            def _(e):
                run(e, "dve")

            @block.gpsimd
            def _(e):
                run(e, "pool")

            @block.sync
            def _(e):
                run(e, "sp")


import contextlib
import ml_dtypes

NT = 33
TOK = NT * 128
DM = 1024
DFF = 2816
NFC = 22
GROUPS = ((128, 1), (512, 4), (2048, 16))
EPS = 1e-6


class Ctx:
    pass


def rms_to_hT(P, C, xt, xkey, gbc, gkey, hb, hbkey, hT_dst, hTkeys, pT, ss, sskey, junk):
    P.act(M("activation", out=junk, in_=xt, func=AF.Square), reads=[xkey], writes=["junk"])
    P.dve(M("tensor_reduce", out=ss[:, 0:1], in_=junk, axis=AX.X, op=ALU.add), reads=["junk"], writes=[(sskey, 0)])
    P.dve(M("tensor_scalar", out=ss[:, 1:2], in0=ss[:, 0:1], scalar1=1.0 / DM, scalar2=EPS,
                                    op0=ALU.mult, op1=ALU.add), reads=[(sskey, 0)], writes=[(sskey, 1)])
    P.act(M("activation", out=ss[:, 2:3], in_=ss[:, 1:2], func=AF.Sqrt), reads=[(sskey, 1)], writes=[(sskey, 2)])
    P.dve(M("reciprocal", out=ss[:, 3:4], in_=ss[:, 2:3]), reads=[(sskey, 2)], writes=[(sskey, 3)])
    P.dve(M("scalar_tensor_tensor", out=hb[:], in0=xt, scalar=ss[:, 3:4], in1=gbc, op0=ALU.mult, op1=ALU.mult),
          reads=[xkey, (sskey, 3), gkey], writes=[hbkey])
    for kc in range(8):
        P.pe(M("transpose", out=pT[:, kc, :], in_=hb[:, kc * 128:(kc + 1) * 128], identity=C.idb[:]),
             reads=[hbkey, "idb"], writes=["pT"])
    P.act(M("copy", out=hT_dst, in_=pT[:]), reads=["pT"], writes=hTkeys)


def load_w_bf16(P, key, dst, src_view, nchunk, wkey):
    for c in range(nchunk):
        P.dma(key, dst[:, c, :], src_view[:, c, :], writes=([wkey + (c_,) for c_ in range(nchunk)] if c == nchunk - 1 else []), eng="pool", max_dma_last_dim=4096)


def ffn_phase(P, C, nc, src, skey, dst, dkey, wg, wu, wd, gvec, tag, final_out=None):
    with contextlib.ExitStack() as st:
        sb = lambda name, shape, dt: st.enter_context(nc.sbuf_tensor(tag + name, shape, dt))
        ps = lambda name, shape, dt: st.enter_context(nc.psum_tensor(tag + name, shape, dt))
        wgs = sb("wg", [128, 8, DFF], BF16)
        wus = sb("wu", [128, 8, DFF], BF16)
        wds = sb("wd", [128, NFC, DM], BF16)
        gbc = sb("gbc", [128, DM], F32)
        xt = [sb("xt%d" % i, [128, DM], F32) for i in range(2)]
        junk = sb("junk", [128, DM], F32)
        ss = [sb("ss%d" % i, [128, 4], F32) for i in range(2)]
        hb = [sb("hb%d" % i, [128, DM], BF16) for i in range(2)]
        hT = [sb("hT%d" % i, [128, 8, 256], BF16) for i in range(2)]
        aT = [sb("aT%d" % i, [128, NFC, 256], BF16) for i in range(2)]
        sg = [sb("sg%d" % i, [128, 256], BF16) for i in range(2)]
        xr = [sb("xr%d" % i, [128, DM], F32) for i in range(2)]
        pT = ps("pT", [128, 8, 128], BF16)
        pG = [ps("pG%d" % i, [128, 512], F32) for i in range(2)]
        pU = [ps("pU%d" % i, [128, 512], F32) for i in range(2)]
        pD = [ps("pD%d" % i, [128, 512], F32) for i in range(2)]
        K = lambda *a: (tag,) + a
        load_w_bf16(P, tag + "wg", wgs, wg.rearrange("(kc p) n -> p kc n", p=128), 8, K("wg"))
        load_w_bf16(P, tag + "wu", wus, wu.rearrange("(kc p) n -> p kc n", p=128), 8, K("wu"))
        load_w_bf16(P, tag + "wd", wds, wd.rearrange("(kc p) n -> p kc n", p=128), NFC, K("wd"))
        P.dma(tag + "g", gbc[:], gvec.partition_broadcast(128), writes=[K("gbc")])
        nmt = (NT + 1) // 2
        for mt in range(nmt):
            tiles = [t for t in (2 * mt, 2 * mt + 1) if t < NT]
            ms = mt % 2
            ncol = 128 * len(tiles)
            for i, t in enumerate(tiles):
                s = t % 2
                P.dma(tag + "x%d" % s, xt[s][:], src[t * 128:(t + 1) * 128, :], reads=[("dram", skey, t)],
                      writes=[K("xt", s)])
                rms_to_hT(P, C, xt[s][:], K("xt", s), gbc[:], K("gbc"), hb[s], K("hb", s),
                          hT[ms][:, :, i * 128:(i + 1) * 128], [K("hT", ms, i)], pT, ss[s], K("ss", s), junk[:])
            hkeys = [K("hT", ms, i) for i in range(len(tiles))]
            for fc in range(NFC):
                q = fc % 2
                for kc in range(8):
                    P.pe(M("matmul", pG[q][:, 0:ncol], lhsT=wgs[:, kc, fc * 128:(fc + 1) * 128],
                                                              rhs=hT[ms][:, kc, 0:ncol], start=(kc == 0), stop=(kc == 7)),
                         reads=hkeys + [K("wg", kc)], writes=[K("pG", q)])
                for kc in range(8):
                    P.pe(M("matmul", pU[q][:, 0:ncol], lhsT=wus[:, kc, fc * 128:(fc + 1) * 128],
                                                              rhs=hT[ms][:, kc, 0:ncol], start=(kc == 0), stop=(kc == 7)),
                         reads=hkeys + [K("wu", kc)], writes=[K("pU", q)])
                P.act(M("activation", out=sg[q][:, 0:ncol], in_=pG[q][:, 0:ncol], func=AF.Silu),
                      reads=[K("pG", q)], writes=[K("sg", q)])
                P.dve(M("tensor_tensor", out=aT[ms][:, fc, 0:ncol], in0=sg[q][:, 0:ncol], in1=pU[q][:, 0:ncol], op=ALU.mult),
                      reads=[K("sg", q), K("pU", q)], writes=[K("aT", ms, fc)])
            for i, t in enumerate(tiles):
                s = t % 2
                P.dma(tag + "xr%d" % s, xr[s][:], src[t * 128:(t + 1) * 128, :], reads=[("dram", skey, t)], writes=[K("xr", s)])
                for half in range(2):
                    for fc in range(NFC):
                        P.pe(M("matmul", pD[half][:], lhsT=aT[ms][:, fc, i * 128:(i + 1) * 128],
                                                                       rhs=wds[:, fc, half * 512:(half + 1) * 512],
                                                                       start=(fc == 0), stop=(fc == NFC - 1)),
                             reads=[K("aT", ms, fc), K("wd", fc)], writes=[K("pD", half)])
                    P.dve(M("tensor_tensor", out=xr[s][:, half * 512:(half + 1) * 512],
                                                                   in0=xr[s][:, half * 512:(half + 1) * 512], in1=pD[half][:], op=ALU.add),
                          reads=[K("xr", s), K("pD", half)], writes=[K("xr", s)])
                if final_out is None:
                    P.dma(tag + "st%d" % s, dst[t * 128:(t + 1) * 128, :], xr[s][:], reads=[K("xr", s)], writes=[("dram", dkey, t)])
                else:
                    yp, ys = final_out
                    if t < 32:
                        P.dma(tag + "st%d" % s, yp[t * 128:(t + 1) * 128, :], xr[s][:], reads=[K("xr", s)])
                    else:
                        P.dma(tag + "st%d" % s, ys[:, :], xr[s][0:32, :], reads=[K("xr", s)])


def attn_a1(P, C, nc, D):
    tag = "a1"
    K = lambda *a: (tag,) + a
    with contextlib.ExitStack() as st:
        sb = lambda name, shape, dt: st.enter_context(nc.sbuf_tensor(tag + name, shape, dt))
        ps = lambda name, shape, dt: st.enter_context(nc.psum_tensor(tag + name, shape, dt))
        wq = sb("wq", [128, 8, 4608], BF16)
        gbc = sb("gbc", [128, DM], F32)
        gains = sb("gains", [128, 2, 2, 32], F32)
        xt = [sb("xt%d" % i, [128, DM], F32) for i in range(2)]
        junk = sb("junk", [128, 3072], F32)
        jn = sb("jn", [128, DM], F32)
        ss = [sb("ss%d" % i, [128, 4], F32) for i in range(2)]
        hb = [sb("hb%d" % i, [128, DM], BF16) for i in range(2)]
        hT = [sb("hT%d" % i, [128, 8, 128], BF16) for i in range(2)]
        qkf = sb("qkf", [128, 3072], F32)
        kr = [sb("kr%d" % i, [128, 3072], F32) for i in range(2)]
        vf = [sb("vf%d" % i, [128, 1536], F32) for i in range(2)]
        vb = [sb("vb%d" % i, [128, 1536], BF16) for i in range(2)]
        qkb = sb("qkb", [128, 3072], BF16)
        qkTs = [sb("qkTs%d" % i, [128, 24, 128], BF16) for i in range(2)]
        ssq = sb("ssq", [128, 4, 48], F32)
        cs = [sb("cs%d" % i, [128, 2, 32], F32) for i in range(2)]
        tabs = sb("tabs", [128, 2, 2, 2, 32], F32)
        ra = sb("ra", [128, 2, 24, 32], F32)
        rb = sb("rb", [128, 2, 24, 32], F32)
        ra2 = sb("ra2", [128, 2, 24, 32], F32)
        rb2 = sb("rb2", [128, 2, 24, 32], F32)
        pT = ps("pT", [128, 8, 128], BF16)
        pQ = [ps("pQ%d" % i, [128, 512], F32) for i in range(4)]
        pT2 = [ps("pT2%d" % i, [128, 8, 128], BF16) for i in range(2)]

        load_w_bf16(P, "a1w", wq, D.w_qkv.rearrange("(kc p) n -> p kc n", p=128), 8, K("wq"))
        P.dma("a1g", gbc[:], D.norm_mix0.partition_broadcast(128), writes=[K("gbc")])
        P.dma("a1gq", gains[:, 0, :, :], D.q_norm.partition_broadcast(128), writes=[K("gains", 0)])
        P.dma("a1gk", gains[:, 1, :, :], D.k_norm.partition_broadcast(128), writes=[K("gains", 1)])
        for t in range(NT):
            s = t % 2
            if t == 32:
                P.pool(M("memset", xt[s][:], 0.0), writes=[K("xt", s)])
                P.dma("a1x%d" % s, xt[s][0:32, :], D.xs[:, :], writes=[K("xt", s)])
            else:
                P.dma("a1x%d" % s, xt[s][:], D.xp[t * 128:(t + 1) * 128, :], writes=[K("xt", s)])
            P.dma("a1cs%d" % s, cs[s][:], D.rope[t], writes=[K("cs", s)])
            rms_to_hT(P, C, xt[s][:], K("xt", s), gbc[:], K("gbc"), hb[s], K("hb", s),
                      hT[s][:], [K("hT", s)], pT, ss[s], K("ss", s), jn[:])
            for nb in range(9):
                pq = nb % 4
                for kc in range(8):
                    P.pe(M("matmul", pQ[pq][:], lhsT=hT[s][:, kc, :], rhs=wq[:, kc, nb * 512:(nb + 1) * 512],
                                                                   start=(kc == 0), stop=(kc == 7)),
                         reads=[K("hT", s), K("wq", kc)], writes=[K("pQ", pq)])
                if nb < 6:
                    P.act(M("copy", out=qkf[:, nb * 512:(nb + 1) * 512], in_=pQ[pq][:]),
                          reads=[K("pQ", pq)], writes=[K("qkf", nb)])
                    P.act(M("activation", out=junk[:, nb * 512:(nb + 1) * 512], in_=pQ[pq][:], func=AF.Square),
                          reads=[K("pQ", pq)], writes=[("junkq", nb)])
                else:
                    j = nb - 6
                    P.act(M("copy", out=vf[s][:, j * 512:(j + 1) * 512], in_=pQ[pq][:]),
                          reads=[K("pQ", pq)], writes=[K("vf", s, j)])
                    P.act(M("copy", out=vb[s][:, j * 512:(j + 1) * 512], in_=pQ[pq][:]),
                          reads=[K("pQ", pq)], writes=[K("vb", s, j)])
            qkeys = [K("qkf", nb) for nb in range(6)]
            P.dve(M("tensor_reduce", out=ssq[:, 0, :], in_=junk[:].rearrange("p (a b) -> p a b", b=64), axis=AX.X, op=ALU.add),
                  reads=[("junkq", nb) for nb in range(6)], writes=[K("ssq", 0)])
            P.dve(M("tensor_scalar", out=ssq[:, 1, :], in0=ssq[:, 0, :], scalar1=1.0 / 64, scalar2=EPS, op0=ALU.mult, op1=ALU.add),
                  reads=[K("ssq", 0)], writes=[K("ssq", 1)])
            P.act(M("activation", out=ssq[:, 2, :], in_=ssq[:, 1, :], func=AF.Sqrt), reads=[K("ssq", 1)], writes=[K("ssq", 2)])
            P.dve(M("reciprocal", out=ssq[:, 3, :], in_=ssq[:, 2, :]), reads=[K("ssq", 2)], writes=[K("ssq", 3)])
            P.dve(M("tensor_tensor", out=qkf[:].rearrange("p (a b) -> p a b", b=64), in0=qkf[:].rearrange("p (a b) -> p a b", b=64),
                                            in1=ssq[:, 3, :].unsqueeze(2).broadcast_to([128, 48, 64]), op=ALU.mult),
                  reads=qkeys + [K("ssq", 3)], writes=qkeys + [K("qn")])
            for ci in range(2):
                P.pool(M("tensor_tensor", out=tabs[:, ci], in0=gains[:],
                                                             in1=cs[s][:, ci, :].unsqueeze(1).unsqueeze(1).broadcast_to([128, 2, 2, 32]), op=ALU.mult),
                       reads=[K("gains", 0), K("gains", 1), K("cs", s)], writes=[K("tabs", ci)])
            xv = qkf[:].rearrange("p (a g h d) -> p a g h d", a=2, g=24, h=2)
            kv_ = kr[s][:].rearrange("p (a g h d) -> p a g h d", a=2, g=24, h=2)
            x1 = xv[:, :, :, 0, :]
            x2 = xv[:, :, :, 1, :]
            bc = lambda ci, gi: tabs[:, ci, :, gi:gi + 1, :].broadcast_to([128, 2, 24, 32])
            tk = [K("tabs", 0), K("tabs", 1), K("qn")] + qkeys
            P.dve(M("tensor_tensor", out=ra[:], in0=x1, in1=bc(0, 0), op=ALU.mult), reads=tk, writes=[K("ra")])
            P.pool(M("tensor_tensor", out=rb[:], in0=x2, in1=bc(1, 1), op=ALU.mult), reads=tk, writes=[K("rb")])
            P.pool(M("tensor_tensor", out=ra2[:], in0=x2, in1=bc(0, 1), op=ALU.mult), reads=tk, writes=[K("ra2")])
            P.dve(M("tensor_tensor", out=rb2[:], in0=x1, in1=bc(1, 0), op=ALU.mult), reads=tk, writes=[K("rb2")])
            P.dve(M("tensor_tensor", out=kv_[:, :, :, 0, :], in0=ra[:], in1=rb[:], op=ALU.subtract),
                  reads=[K("ra"), K("rb")], writes=[K("kr", s, 0)])
            P.pool(M("tensor_tensor", out=kv_[:, :, :, 1, :], in0=ra2[:], in1=rb2[:], op=ALU.add),
                   reads=[K("ra2"), K("rb2")], writes=[K("kr", s, 1)])
            P.act(M("copy", out=qkb[:], in_=kr[s][:]), reads=[K("kr", s, 0), K("kr", s, 1)], writes=[K("qkb")])
            for grp in range(3):
                pp = grp % 2
                for i in range(8):
                    f = grp * 8 + i
                    P.pe(M("transpose", out=pT2[pp][:, i, :], in_=qkb[:, f * 128:(f + 1) * 128], identity=C.idb[:]),
                         reads=[K("qkb"), "idb"], writes=[K("pT2", pp)])
                P.act(M("copy", out=qkTs[s][:, grp * 8:(grp + 1) * 8, :], in_=pT2[pp][:]),
                      reads=[K("pT2", pp)], writes=[K("qkTs", s, grp)])
            P.dma("a1qk%d" % s, D.QKT[:, :, t * 128:(t + 1) * 128].rearrange("f p t -> p f t"), qkTs[s][:],
                  reads=[K("qkTs", s, g_) for g_ in range(3)], writes=[("dram", "QKT", t)])
            P.dma("a1v%d" % s, D.Vs[t * 128:(t + 1) * 128, :], vb[s][:], reads=[K("vb", s, j) for j in range(3)],
                  writes=[("dram", "Vs", t)])
            krk = [K("kr", s, 0), K("kr", s, 1)]
            vfk = [K("vf", s, j) for j in range(3)]
            for g, (win, dil) in enumerate(GROUPS):
                if t < 32:
                    r0 = t * 128 - (4096 - win)
                    if r0 < 0:
                        continue
                    P.dma("a1ok%d" % s, D.kvp[g][r0:r0 + 128, 0, :], kr[s][:, 1536 + g * 512:1536 + (g + 1) * 512], reads=krk)
                    P.dma("a1ov%d" % s, D.kvp[g][r0:r0 + 128, 1, :], vf[s][:, g * 512:(g + 1) * 512], reads=vfk)
                else:
                    for b in range(4):
                        P.dma("a1ok%d" % s, D.kvs[g][b, win - 8:win, 0, :], kr[s][b * 8:(b + 1) * 8, 1536 + g * 512:1536 + (g + 1) * 512], reads=krk)
                        P.dma("a1ov%d" % s, D.kvs[g][b, win - 8:win, 1, :], vf[s][b * 8:(b + 1) * 8, g * 512:(g + 1) * 512], reads=vfk)
        for g, (win, dil) in enumerate(GROUPS):
            for b in range(4):
                P.dma("a1cc%d" % (b % 2), D.kvs[g][b, 0:win - 8, :, :], D.cache[g][b, 8:win, :, :])


def attn_a2(P, C, nc, D):
    tag = "a2"
    K = lambda *a: (tag,) + a
    with contextlib.ExitStack() as st:
        sb = lambda name, shape, dt: st.enter_context(nc.sbuf_tensor(tag + name, shape, dt))
        ps = lambda name, shape, dt: st.enter_context(nc.psum_tensor(tag + name, shape, dt))
        OT = sb("OT", [64, 8, TOK], BF16)
        mask = sb("mask", [128, 512], BF16)
        ones32 = sb("ones32", [65, 64], F32)
        wo = sb("wo", [64, 8, DM], BF16)
        with contextlib.ExitStack() as st_s:
            sample_attn(P, C, nc, D, st_s, OT, K)
        P.barrier(C.dummy[:])
        st2 = st.enter_context(contextlib.ExitStack())
        sb2 = lambda name, shape, dt: st2.enter_context(nc.sbuf_tensor(tag + name, shape, dt))
        ps2 = lambda name, shape, dt: st2.enter_context(nc.psum_tensor(tag + name, shape, dt))
        qT = [sb2("qT%d" % i, [64, 4096], BF16) for i in range(2)]
        kT = [sb2("kT%d" % i, [64, 4096], BF16) for i in range(2)]
        Va = [sb2("Va%d" % i, [128, 32, 65], BF16) for i in range(2)]
        Oacc = sb2("Oacc", [65, 4096], F32)
        Pb = [sb2("Pb%d" % i, [128, 512], BF16) for i in range(4)]
        pS = [ps2("pS%d" % i, [128, 512], F32) for i in range(3)]
        pO = [ps2("pO%d" % i, [128, 512], F32) for i in range(4)]
        pcount = [0]
        pB = ps2("pB", [128, 512], F32)

        P.dma("a2m", mask[:], D.mask_p, writes=[K("mask")], eng="pool")
        P.pool(M("memset", ones32[:], 1.0), writes=[K("ones32")])
        for i in range(2):
            P.pool(M("memset", Va[i][:, :, 64:65], 1.0), writes=[K("Vone", i)])
        P.dma("a2wo", wo[:], D.w_o.rearrange("(h d) n -> d h n", d=64), writes=[K("wo")], eng="pool")

        for h in range(8):
            for g, (win, dil) in enumerate(GROUPS):
                u = h * 3 + g
                s2 = u % 2
                nkb = 32 // dil
                fq, fk, r0 = g * 4 + h // 2, 12 + g * 4 + h // 2, (h % 2) * 64
                qbuf = qT[s2]
                P.dma("a2q%d" % s2, qbuf[:], D.QKT[fq, r0:r0 + 64, 0:4096], reads=[("dram", "QKT", t) for t in range(32)], writes=[K("qT", s2)])
                P.dma("a2k%d" % s2, kT[s2][:], D.QKT[fk, r0:r0 + 64, 0:4096], reads=[("dram", "QKT", t) for t in range(32)], writes=[K("kT", s2)])
                vview = D.Vs[0:4096, :].rearrange("(kb n r) f -> n r kb f", n=128, r=dil)
                c0 = g * 512 + h * 64
                P.dma("a2v%d" % s2, Va[s2][:, :, 0:64].rearrange("p (r k) d -> p r k d", r=dil), vview[:, :, :, c0:c0 + 64],
                      reads=[("dram", "Vs", t) for t in range(32)], writes=[K("Va", s2, b_) for b_ in range(32)])
                npair = nkb // 2
                vak = [K("Va", s2, b_) for b_ in range(32)] + [K("Vone", s2)]
                for r in range(dil):
                    for kp in range(npair):
                        PB = pcount[0]
                        pcount[0] += 1
                        pp, pbi, pprev = PB % 3, PB % 4, (PB - 1) % 4
                        width = 0
                        for j in range(2):
                            kb = 2 * kp + j
                            nq = 256 if kb < nkb - 1 else 128
                            koff = dil * 128 * kb + r
                            kcols = kT[s2][:, koff:koff + dil * 127 + 1:dil]
                            qcols = qbuf[:, koff:koff + dil * (nq - 1) + 1:dil]
                            P.pe(M("matmul", pS[pp][:, j * 256:j * 256 + nq], lhsT=kcols, rhs=qcols, start=True, stop=True),
                                 reads=[K("kT", s2), K("qT", s2)], writes=[K("pS", pp)])
                            width = j * 256 + nq
                        P.act(M("activation", out=Pb[pbi][:, 0:width], in_=pS[pp][:, 0:width], func=AF.Exp, scale=0.125),
                              reads=[K("pS", pp)], writes=[K("Pb", pbi)])
                        meng = P.pool if PB % 3 == 2 else P.dve
                        meng(M("tensor_tensor", out=Pb[pbi][:, 0:width], in0=Pb[pbi][:, 0:width], in1=mask[:, 0:width], op=ALU.mult),
                             reads=[K("Pb", pbi), K("mask")], writes=[K("Pb", pbi)])
                        for j in range(2):
                            kb = 2 * kp + j
                            koff = dil * 128 * kb + r
                            po = (2 * PB + j) % 4
                            B = r * nkb + kb
                            if kb > 0:
                                prev = Pb[pbi][:, 128:256] if j == 1 else Pb[pprev][:, 384:512]
                                pk = K("Pb", pbi) if j == 1 else K("Pb", pprev)
                                P.pe(M("matmul", pO[po][0:65, 0:128], lhsT=Va[s2][:, B - 1, :], rhs=prev, start=True, stop=False),
                                     reads=vak + [pk], writes=[K("pO", po)])
                            P.pe(M("matmul", pO[po][0:65, 0:128], lhsT=Va[s2][:, B, :], rhs=Pb[pbi][:, j * 256:j * 256 + 128], start=(kb == 0), stop=True),
                                 reads=vak + [K("Pb", pbi)], writes=[K("pO", po)])
                            ocols = Oacc[:, koff:koff + dil * 127 + 1:dil]
                            blks = sorted(set((koff + dil * m) // 128 for m in (0, 127)))
                            blks = list(range(blks[0], blks[-1] + 1))
                            okeys = [K("Oacc", b_) for b_ in blks]
                            if g == 0:
                                P.act(M("copy", out=ocols, in_=pO[po][0:65, 0:128]), reads=[K("pO", po)], writes=okeys)
                            else:
                                P.dve(M("tensor_tensor", out=ocols, in0=ocols, in1=pO[po][0:65, 0:128], op=ALU.add),
                                      reads=[K("pO", po)] + okeys, writes=okeys)
            allk = [K("Oacc", b_) for b_ in range(32)]
            P.dve(M("reciprocal", out=Oacc[64:65, :], in_=Oacc[64:65, :]), reads=allk, writes=[K("rl")])
            for cb in range(8):
                P.pe(M("matmul", pB[0:64, :], lhsT=ones32[64:65, :], rhs=Oacc[64:65, cb * 512:(cb + 1) * 512], start=True, stop=True),
                     reads=[K("rl"), K("ones32")], writes=[K("pB")])
                P.dve(M("tensor_tensor", out=OT[:, h, cb * 512:(cb + 1) * 512], in0=Oacc[0:64, cb * 512:(cb + 1) * 512], in1=pB[0:64, :], op=ALU.mult),
                      reads=[K("pB")] + allk, writes=[K("OT", h, cb)] + [K("Oacc", b_) for b_ in range(cb * 4, cb * 4 + 4)])

        if getattr(D, "dbg_OT", None) is not None:
            P.dma("dbgot", D.dbg_OT, OT[:], reads=[K("OT", h_, c_) for h_ in range(8) for c_ in range(8)] + [K("OTs")], eng="pool", max_dma_last_dim=2048)
        st2.close()
        P.barrier(C.dummy[:])
        xt = [sb("xt%d" % i, [128, DM], F32) for i in range(2)]
        pW = [ps("pW%d" % i, [128, 512], F32) for i in range(2)]
        for t in range(NT):
            s = t % 2
            if t == 32:
                P.pool(M("memset", xt[s][:], 0.0), writes=[K("xt", s)])
                P.dma("a2x%d" % s, xt[s][0:32, :], D.xs[:, :], writes=[K("xt", s)])
                otk = [K("OTs")]
            else:
                P.dma("a2x%d" % s, xt[s][:], D.xp[t * 128:(t + 1) * 128, :], writes=[K("xt", s)])
                otk = [K("OT", h, t // 4) for h in range(8)]
            for half in range(2):
                for h in range(8):
                    P.pe(M("matmul", pW[half][:], lhsT=OT[:, h, t * 128:(t + 1) * 128], rhs=wo[:, h, half * 512:(half + 1) * 512],
                                                                start=(h == 0), stop=(h == 7)),
                         reads=otk + [K("wo")], writes=[K("pW", half)])
                P.dve(M("tensor_tensor", out=xt[s][:, half * 512:(half + 1) * 512], in0=xt[s][:, half * 512:(half + 1) * 512],
                                                               in1=pW[half][:], op=ALU.add),
                      reads=[K("xt", s), K("pW", half)], writes=[K("xt", s)])
            P.dma("a2st%d" % s, D.X1[t * 128:(t + 1) * 128, :], xt[s][:], reads=[K("xt", s)], writes=[("dram", "X1", t)])


def sample_attn(P, C, nc, D, st, OT, K):
    tag = "sa"
    sb = lambda name, shape, dt: st.enter_context(nc.sbuf_tensor(tag + name, shape, dt))
    ps = lambda name, shape, dt: st.enter_context(nc.psum_tensor(tag + name, shape, dt))
    qs = sb("qs", [128, 12, 32], BF16)
    ks = sb("ks", [128, 12, 32], BF16)
    vnew = [sb("vnew%d" % b, [8, 24, 65], BF16) for b in range(4)]
    kcf = sb("kcf", [128, 13, 512], F32)
    vcf = sb("vcf", [128, 13, 512], F32)
    kcb = sb("kcb", [128, 13, 512], BF16)
    kcT = sb("kcT", [128, 13, 4, 128], BF16)
    vca = sb("vca", [128, 13, 8, 65], BF16)
    msk = sb("msk", [128, 13, 8], BF16)
    mskn = sb("mskn", [8, 3, 8], BF16)
    Ps = sb("Ps", [128, 13, 8], BF16)
    Pn = sb("Pn", [8, 3, 8], BF16)
    osb = sb("osb", [8, 8, 64], BF16)
    rl = sb("rl", [8, 2], F32)
    pT = ps("pT", [128, 8, 128], BF16)
    pS = ps("pS", [128, 512], F32)[:, 0:104].rearrange("p (k q) -> p k q", q=8)
    pN = ps("pN", [128, 512], F32)[0:8, 0:24].rearrange("p (k q) -> p k q", q=8)
    pO = ps("pO", [128, 512], F32)[0:8, 0:65]
    pX = ps("pX", [128, 1024], BF16)[0:64, 0:64].rearrange("p (k q) -> p k q", q=8)
    SK = lambda *a: ("sa",) + a
    qk_r = [("dram", "QKT", 32)]
    P.dma("saq", qs[:], D.QKT[0:12, :, 4096:4128].rearrange("f p t -> p f t"), reads=qk_r, writes=[SK("qs")])
    P.dma("sak", ks[:], D.QKT[12:24, :, 4096:4128].rearrange("f p t -> p f t"), reads=qk_r, writes=[SK("ks")])
    P.dma("sam", msk[:], D.mask_s, writes=[SK("msk")], eng="pool")
    P.dma("samn", mskn[:], D.mask_n, writes=[SK("mskn")], eng="pool")
    P.pool(M("memset", vca[:, :, :, 64:65], 1.0), writes=[SK("vone")])
    for b in range(4):
        P.pool(M("memset", vnew[b][:, :, 64:65], 1.0), writes=[SK("vnone", b)])
        P.dma("savn%d" % b, vnew[b][:, :, 0:64], D.Vs[4096 + 8 * b:4096 + 8 * b + 8, :].rearrange("p (a d) -> p a d", d=64),
              reads=[("dram", "Vs", 32)], writes=[SK("vnew", b)])
    for b in range(4):
        for kv_i, dst in ((0, kcf), (1, vcf)):
            P.dma("sac%da" % kv_i, dst[:, 0, :], D.cache[0][b, :, kv_i, :], writes=[SK("cf", kv_i, 0)])
            P.dma("sac%db" % kv_i, dst[:, 1:5, :], D.cache[1][b, :, kv_i, :].rearrange("(k n) f -> n k f", n=128), writes=[SK("cf", kv_i, 1)])
            P.dma("sac%dc" % kv_i, dst[:, 5:13, :], D.cache[2][b, :, kv_i, :].rearrange("(n r) f -> n r f", r=16)[:, 0:8, :], writes=[SK("cf", kv_i, 2)])
        cfk = [SK("cf", 0, i) for i in range(3)]
        cfv = [SK("cf", 1, i) for i in range(3)]
        P.act(M("copy", out=kcb[:], in_=kcf[:]), reads=cfk, writes=[SK("kcb")])
        P.pool(M("tensor_copy", out=vca[:, :, :, 0:64], in_=vcf[:].rearrange("p k (h d) -> p k h d", d=64)), reads=cfv, writes=[SK("vca")])
        for grp in range(7):
            items = [(blk, hp) for blk in range(13) for hp in range(4)][grp * 8:(grp + 1) * 8]
            for i, (blk, hp) in enumerate(items):
                P.pe(M("transpose", out=pT[:, i, :], in_=kcb[:, blk, hp * 128:(hp + 1) * 128], identity=C.idb[:]),
                     reads=[SK("kcb"), "idb"], writes=[SK("pT")])
            n = len(items)
            b0, h0 = items[0]
            dstv = kcT[:].rearrange("p k h n -> p (k h) n")[:, grp * 8:grp * 8 + n, :]
            P.act(M("copy", out=dstv, in_=pT[:, 0:n, :]), reads=[SK("pT")], writes=[SK("kcT", grp)])
        kck = [SK("kcT", grp) for grp in range(7)]
        for h in range(8):
            pr = (h % 2) * 64
            hp = h // 2
            gof = lambda blk: 0 if blk == 0 else (1 if blk < 5 else 2)
            for blk in range(13):
                g = gof(blk)
                P.pe(M("matmul", pS[:, blk, :], lhsT=kcT[pr:pr + 64, blk, hp, :], rhs=qs[pr:pr + 64, g * 4 + hp, 8 * b:8 * b + 8],
                                                                         start=True, stop=True),
                     reads=kck + [SK("qs")], writes=[SK("pS")])
            for g in range(3):
                P.pe(M("matmul", pN[:, g, :], lhsT=ks[pr:pr + 64, g * 4 + hp, 8 * b:8 * b + 8], rhs=qs[pr:pr + 64, g * 4 + hp, 8 * b:8 * b + 8],
                                                              start=True, stop=True),
                     reads=[SK("ks"), SK("qs")], writes=[SK("pN")])
            P.act(M("activation", out=Ps[:], in_=pS[:], func=AF.Exp, scale=0.125), reads=[SK("pS")], writes=[SK("Ps")])
            P.act(M("activation", out=Pn[:], in_=pN[:], func=AF.Exp, scale=0.125), reads=[SK("pN")], writes=[SK("Pn")])
            P.dve(M("tensor_tensor", out=Ps[:], in0=Ps[:], in1=msk[:], op=ALU.mult), reads=[SK("Ps"), SK("msk")], writes=[SK("Ps")])
            P.dve(M("tensor_tensor", out=Pn[:], in0=Pn[:], in1=mskn[:], op=ALU.mult), reads=[SK("Pn"), SK("mskn")], writes=[SK("Pn")])
            for blk in range(13):
                P.pe(M("matmul", pO[:], lhsT=Ps[:, blk, :], rhs=vca[:, blk, h, :], start=(blk == 0), stop=False),
                     reads=[SK("Ps"), SK("vca"), SK("vone")], writes=[SK("pO")])
            for g in range(3):
                P.pe(M("matmul", pO[:], lhsT=Pn[:, g, :], rhs=vnew[b][:, g * 8 + h, :], start=False, stop=(g == 2)),
                     reads=[SK("Pn"), SK("vnew", b), SK("vnone", b)], writes=[SK("pO")])
            P.dve(M("reciprocal", out=rl[:, 0:1], in_=pO[:, 64:65]), reads=[SK("pO")], writes=[SK("rl")])
            P.dve(M("tensor_scalar", out=osb[:, h, :], in0=pO[:, 0:64], scalar1=rl[:, 0:1], scalar2=None, op0=ALU.mult),
                  reads=[SK("pO"), SK("rl")], writes=[SK("osb", h)])
        for h in range(8):
            P.pe(M("transpose", out=pX[:, h, :], in_=osb[:, h, :], identity=C.idb[0:8, 0:8]),
                 reads=[SK("osb", h_) for h_ in range(8)] + ["idb"], writes=[SK("pX")])
        P.act(M("copy", out=OT[:, :, 4096 + 8 * b:4096 + 8 * b + 8], in_=pX[:]), reads=[SK("pX")], writes=[K("OTs")])
    P.pool(M("memset", OT[:, :, 4128:TOK], 0.0), writes=[K("OTs")])


SEG_T = 4
NCH = 64
TWO_PI = 6.283185307179586
PI = 3.141592653589793


def ssm_phase(P, C, nc, D):
    tag = "ss"
    K = lambda *a: (tag,) + a
    with contextlib.ExitStack() as st:
        sb = lambda name, shape, dt: st.enter_context(nc.sbuf_tensor(tag + name, shape, dt))
        ps = lambda name, shape, dt: st.enter_context(nc.psum_tensor(tag + name, shape, dt))
        Wp2 = sb("Wp2", [128, 8, 8, 2, 2, 64], BF16)
        Vp = sb("Vp", [128, 32, 9, 2, 2, 16], BF16)
        Kblk = sb("Kblk", [128, 8, 8, 128], BF16)
        Aa = sb("Aa", [128, 2, 32], F32)
        Ab = sb("Ab", [128, 2, 32], F32)
        cnt = [0]

        def dv(name, *a, reads=(), writes=(), eng="dve", **k):
            return P.add(eng, M(name, *a, **k), reads, writes)

        with contextlib.ExitStack() as stp:
            sbp = lambda name, shape, dt: stp.enter_context(nc.sbuf_tensor(tag + "p" + name, shape, dt))
            psp = lambda name, shape, dt: stp.enter_context(nc.psum_tensor(tag + "p" + name, shape, dt))

            def lam_E(sfx, are_d, aim_d, ldt_d, shape, sbp):
                k_ = lambda n: K(sfx, n)
                t = {}
                for n in ("are", "aim", "ldt", "dt", "x1", "mag", "ang", "kf", "y", "m", "y2", "sin", "cos", "Lr", "Li",
                          "den", "t1", "t2", "nr", "Er", "Ei"):
                    t[n] = sbp(sfx + n, shape, F32)
                ki = sbp(sfx + "ki", shape, mybir.dt.int32)
                P.dma("ssp" + sfx + "a", t["are"][:], are_d, writes=[k_("are")])
                P.dma("ssp" + sfx + "b", t["aim"][:], aim_d, writes=[k_("aim")])
                P.dma("ssp" + sfx + "c", t["ldt"][:], ldt_d, writes=[k_("ldt")])
                P.act(M("activation", out=t["dt"][:], in_=t["ldt"][:], func=AF.Exp), reads=[k_("ldt")], writes=[k_("dt")])
                tt = lambda o, a, b, op: dv("tensor_tensor", out=t[o][:], in0=t[a][:], in1=t[b][:], op=op, reads=[k_(a), k_(b)], writes=[k_(o)])
                tt("x1", "are", "dt", ALU.mult)
                P.act(M("activation", out=t["mag"][:], in_=t["x1"][:], func=AF.Exp), reads=[k_("x1")], writes=[k_("mag")])
                tt("ang", "aim", "dt", ALU.mult)
                dv("tensor_scalar", out=t["kf"][:], in0=t["ang"][:], scalar1=1.0 / TWO_PI, scalar2=None, op0=ALU.mult, reads=[k_("ang")], writes=[k_("kf")])
                dv("tensor_copy", out=ki[:], in_=t["kf"][:], reads=[k_("kf")], writes=[k_("ki")])
                dv("tensor_copy", out=t["kf"][:], in_=ki[:], reads=[k_("ki")], writes=[k_("kf")])
                dv("scalar_tensor_tensor", out=t["y"][:], in0=t["kf"][:], scalar=-TWO_PI, in1=t["ang"][:], op0=ALU.mult, op1=ALU.add,
                   reads=[k_("kf"), k_("ang")], writes=[k_("y")])

                def wrap(yk):
                    dv("tensor_single_scalar", out=t["m"][:], in_=t[yk][:], scalar=PI, op=ALU.is_gt, reads=[k_(yk)], writes=[k_("m")])
                    dv("scalar_tensor_tensor", out=t[yk][:], in0=t["m"][:], scalar=-TWO_PI, in1=t[yk][:], op0=ALU.mult, op1=ALU.add,
                       reads=[k_("m"), k_(yk)], writes=[k_(yk)])
                    dv("tensor_single_scalar", out=t["m"][:], in_=t[yk][:], scalar=-PI, op=ALU.is_lt, reads=[k_(yk)], writes=[k_("m")])
                    dv("scalar_tensor_tensor", out=t[yk][:], in0=t["m"][:], scalar=TWO_PI, in1=t[yk][:], op0=ALU.mult, op1=ALU.add,
                       reads=[k_("m"), k_(yk)], writes=[k_(yk)])
                wrap("y")
                dv("tensor_scalar", out=t["y2"][:], in0=t["y"][:], scalar1=PI / 2, scalar2=None, op0=ALU.add, reads=[k_("y")], writes=[k_("y2")])
                wrap("y2")
                P.act(M("activation", out=t["sin"][:], in_=t["y"][:], func=AF.Sin), reads=[k_("y")], writes=[k_("sin")])
                P.act(M("activation", out=t["cos"][:], in_=t["y2"][:], func=AF.Sin), reads=[k_("y2")], writes=[k_("cos")])
                tt("Lr", "mag", "cos", ALU.mult)
                tt("Li", "mag", "sin", ALU.mult)
                tt("t1", "are", "are", ALU.mult)
                tt("t2", "aim", "aim", ALU.mult)
                tt("den", "t1", "t2", ALU.add)
                dv("reciprocal", out=t["den"][:], in_=t["den"][:], reads=[k_("den")], writes=[k_("den")])
                dv("tensor_scalar", out=t["nr"][:], in0=t["Lr"][:], scalar1=-1.0, scalar2=None, op0=ALU.add, reads=[k_("Lr")], writes=[k_("nr")])
                tt("t1", "nr", "are", ALU.mult)
                tt("t2", "Li", "aim", ALU.mult)
                tt("Er", "t1", "t2", ALU.add)
                tt("Er", "Er", "den", ALU.mult)
                tt("t1", "Li", "are", ALU.mult)
                tt("t2", "nr", "aim", ALU.mult)
                tt("Ei", "t1", "t2", ALU.subtract)
                tt("Ei", "Ei", "den", ALU.mult)
                return t, k_

            def cmul(o_r, o_i, a_r, a_i, b_r, b_i, tmp1, tmp2):
                def tt(o, a, b, op):
                    dv("tensor_tensor", out=o[0], in0=a[0], in1=b[0], op=op, reads=[a[1], b[1]], writes=[o[1]])
                tt(tmp1, a_r, b_r, ALU.mult)
                tt(tmp2, a_i, b_i, ALU.mult)
                tt(o_r, tmp1, tmp2, ALU.subtract)
                tt(tmp1, a_r, b_i, ALU.mult)
                tt(tmp2, a_i, b_r, ALU.mult)
                tt(o_i, tmp1, tmp2, ALU.add)

            tB, kB = lam_E("B", D.s_are_B, D.s_aim_B, D.s_ldt_B, [128, 512], sbp)
            bB = sbp("bB", [128, 2, 512], F32)
            P.dma("sspbB0", bB[:, 0, :], D.s_bre_B, writes=[K("bB", 0)])
            P.dma("sspbB1", bB[:, 1, :], D.s_bim_B, writes=[K("bB", 1)])
            Wall = sbp("Wall", [128, 8, 2, 512], F32)
            tm1 = sbp("tm1", [128, 512], F32)
            tm2 = sbp("tm2", [128, 512], F32)
            pm = sbp("pm", [128, 2], F32)
            P.dma("ssppm", pm[:], D.s_pm, writes=[K("pm")])
            T1, T2 = (tm1[:], K("tm1")), (tm2[:], K("tm2"))
            cmul((Wall[:, 0, 0, :], K("W", 0, 0)), (Wall[:, 0, 1, :], K("W", 0, 1)), (tB["Er"][:], kB("Er")), (tB["Ei"][:], kB("Ei")),
                 (bB[:, 0, :], K("bB", 0)), (bB[:, 1, :], K("bB", 1)), T1, T2)
            for k in range(1, 8):
                cmul((Wall[:, k, 0, :], K("W", k, 0)), (Wall[:, k, 1, :], K("W", k, 1)), (tB["Lr"][:], kB("Lr")), (tB["Li"][:], kB("Li")),
                     (Wall[:, k - 1, 0, :], K("W", k - 1, 0)), (Wall[:, k - 1, 1, :], K("W", k - 1, 1)), T1, T2)
            for k in range(8):
                for ri in range(2):
                    for par in range(2):
                        if (k + ri + par) % 2:
                            P.act(M("activation", out=Wp2[:, :, k, ri, par, :], in_=Wall[:, k, ri, :].rearrange("p (f q) -> p f q", q=64),
                                    func=AF.Copy, scale=pm[:, par:par + 1]), reads=[K("W", k, ri), K("pm")], writes=[K("Wp2")])
                        else:
                            dv("tensor_scalar", out=Wp2[:, :, k, ri, par, :], in0=Wall[:, k, ri, :].rearrange("p (f q) -> p f q", q=64),
                               scalar1=pm[:, par:par + 1], scalar2=None, op0=ALU.mult,
                               reads=[K("W", k, ri), K("pm")], writes=[K("Wp2")])
        P.barrier(C.dummy[:])
        with contextlib.ExitStack() as stp:
            sbp = lambda name, shape, dt: stp.enter_context(nc.sbuf_tensor(tag + "q" + name, shape, dt))
            psp = lambda name, shape, dt: stp.enter_context(nc.psum_tensor(tag + "q" + name, shape, dt))
            tC, kC = lam_E("C", D.s_are_C, D.s_aim_C, D.s_ldt_C, [128, 32], sbp)
            bC = sbp("bC", [128, 2, 32, 16], F32)
            P.dma("sspbC0", bC[:, 0], D.s_bre_C, writes=[K("bC", 0)])
            P.dma("sspbC1", bC[:, 1], D.s_bim_C, writes=[K("bC", 1)])
            Vall = sbp("Vall", [128, 9, 2, 32, 16], F32)
            P.dma("sspcC0", Vall[:, 0, 0], D.s_cre_C, writes=[K("V", 0, 0)])
            P.dma("sspcC1", Vall[:, 0, 1], D.s_cim_C, writes=[K("V", 0, 1)])
            Bb = sbp("Bb", [128, 2, 32, 16], F32)
            tc1 = sbp("tc1", [128, 32, 16], F32)
            tc2 = sbp("tc2", [128, 32, 16], F32)
            qm = sbp("qm", [128, 2], F32)
            P.dma("sspqm", qm[:], D.s_qm, writes=[K("qm")])
            U1, U2 = (tc1[:], K("tc1")), (tc2[:], K("tc2"))
            bc16 = lambda ap: ap.unsqueeze(2).broadcast_to([128, 32, 16])
            cmul((Bb[:, 0], K("Bb", 0)), (Bb[:, 1], K("Bb", 1)), (bC[:, 0], K("bC", 0)), (bC[:, 1], K("bC", 1)),
                 (bc16(tC["Er"][:]), kC("Er")), (bc16(tC["Ei"][:]), kC("Ei")), U1, U2)
            for k in range(1, 9):
                cmul((Vall[:, k, 0], K("V", k, 0)), (Vall[:, k, 1], K("V", k, 1)), (Vall[:, k - 1, 0], K("V", k - 1, 0)), (Vall[:, k - 1, 1], K("V", k - 1, 1)),
                     (bc16(tC["Lr"][:]), kC("Lr")), (bc16(tC["Li"][:]), kC("Li")), U1, U2)
            Bp = sbp("Bp", [128, 32, 2, 2, 16], BF16)
            for ri in range(2):
                for par in range(2):
                    dv("tensor_scalar", out=Bp[:, :, ri, par, :], in0=Bb[:, ri], scalar1=qm[:, par:par + 1], scalar2=None, op0=ALU.mult,
                       reads=[K("Bb", ri), K("qm")], writes=[K("Bp")])
                    for k in range(9):
                        dv("tensor_scalar", out=Vp[:, :, k, ri, par, :], in0=Vall[:, k, ri], scalar1=qm[:, par:par + 1],
                           scalar2=(1.0 if ri == 0 else -1.0), op0=ALU.mult, op1=ALU.mult,
                           reads=[K("V", k, ri), K("qm")], writes=[K("Vp")], eng="dve")
            L2 = sbp("L2", [128, 3, 2, 32], F32)
            ta1 = sbp("ta1", [128, 32], F32)
            ta2 = sbp("ta2", [128, 32], F32)
            A1_, A2_ = (ta1[:], K("ta1")), (ta2[:], K("ta2"))
            prev = ((tC["Lr"][:], kC("Lr")), (tC["Li"][:], kC("Li")))
            for i in range(3):
                cur = ((L2[:, i, 0, :], K("L2", i, 0)), (L2[:, i, 1, :], K("L2", i, 1)))
                cmul(cur[0], cur[1], prev[0], prev[1], prev[0], prev[1], A1_, A2_)
                prev = cur
            a8k = [K("L2", 2, 0), K("L2", 2, 1)]
            dv("tensor_copy", out=Aa[:, 0, :], in_=L2[:, 2, 0, :], reads=a8k, writes=[K("Aa")])
            dv("tensor_copy", out=Aa[:, 1, :], in_=L2[:, 2, 0, :], reads=a8k, writes=[K("Aa")])
            dv("tensor_scalar", out=Ab[:, 0, :], in0=L2[:, 2, 1, :], scalar1=-1.0, scalar2=None, op0=ALU.mult, reads=a8k, writes=[K("Ab")])
            dv("tensor_copy", out=Ab[:, 1, :], in_=L2[:, 2, 1, :], reads=a8k, writes=[K("Ab")])
            Kf = sbp("Kf", [128, 8, 128], F32)
            idf = sbp("idf", [128, 128], F32)
            dL = sbp("dL", [128, 8], F32)
            P.dma("sspid", idf[:], D.ident, writes=[K("idf")])
            P.dma("sspd", dL[:], D.s_d, writes=[K("dL")])
            pK = psp("pK", [128, 8, 32], F32)
            P.pool(M("memset", Kf[:], 0.0), writes=[K("Kf")])
            for fc in range(8):
                for q4 in range(4):
                    pr = 4 * fc + q4
                    for tau in range(8):
                        for ri in range(2):
                            P.pe(M("matmul", pK[32 * q4:32 * q4 + 32, tau, :], lhsT=Bp[:, pr, ri].rearrange("p a c -> p (a c)"),
                                   rhs=Vp[:, pr, tau, ri].rearrange("p a c -> p (a c)"), start=(ri == 0), stop=(ri == 1), tile_position=(0, 32 * q4)),
                                 reads=[K("Bp"), K("Vp")], writes=[K("pK")])
                    dv("tensor_copy", out=Kf[32 * q4:32 * q4 + 32, :, 32 * q4:32 * q4 + 32], in_=pK[32 * q4:32 * q4 + 32, :, :],
                       reads=[K("pK")], writes=[K("Kf")])
                dv("scalar_tensor_tensor", out=Kf[:, 0, :], in0=idf[:], scalar=dL[:, fc:fc + 1], in1=Kf[:, 0, :], op0=ALU.mult, op1=ALU.add,
                   reads=[K("Kf"), K("idf"), K("dL")], writes=[K("Kf")])
                P.act(M("copy", out=Kblk[:, fc], in_=Kf[:]), reads=[K("Kf")], writes=[K("Kblk", fc)])
        P.barrier(C.dummy[:])

        win = sb("win", [128, 8, DM], BF16)
        gbc = sb("gbc", [128, DM], F32)
        load_w_bf16(P, "ssw1", win, D.ssm_w_in.rearrange("(kc p) n -> p kc n", p=128), 8, K("win"))
        P.dma("ssg", gbc[:], D.norm_mix1.partition_broadcast(128), writes=[K("gbc")])
        xt = [sb("xt%d" % i, [128, DM], F32) for i in range(2)]
        junk = sb("junk", [128, DM], F32)
        ss = [sb("ss%d" % i, [128, 4], F32) for i in range(2)]
        hb = [sb("hb%d" % i, [128, DM], BF16) for i in range(2)]
        hT = [sb("hT%d" % i, [128, 8, 128], BF16) for i in range(2)]
        uT2 = [sb("uT%d" % i, [128, 8, SEG_T * 128], BF16) for i in range(2)]
        gT2 = [sb("gT0", [128, 8, SEG_T * 128], BF16)] * 2
        xs2 = [sb("xs%d" % i, [128, 2, 32, NCH + 1], F32) for i in range(2)]
        xhb2 = [sb("xhb%d" % i, [128, 2, 32, NCH], BF16) for i in range(2)]
        xs_s = sb("xs_s", [128, 2, 32, 2, 4], F32)
        xhb_s = sb("xhb_s", [128, 2, 32, 4], BF16)
        r1 = sb("r1", [128, 2, 32], F32)
        r2 = sb("r2", [128, 2, 32], F32)
        g1 = sb("g1", [128, 512], F32)
        g2 = sb("g2", [128, 512], F32)
        pT = ps("pT", [128, 8, 128], BF16)
        pU = [ps("pU%d" % i, [128, 4, 128], F32) for i in range(2)]
        pSs = [ps("pSs%d" % i, [128, 512], F32) for i in range(4)]
        pY = ps("pY", [128, 512], F32)
        P.pool(M("memset", xs2[0][:, :, :, 0:1], 0.0), writes=[K("xs", 0)])
        P.dma("ssst0", xs_s[:, :, :, 0, :], D.st0, writes=[K("xs_s", 0)])

        def segment(tiles, nch, sample, sgi):
            ncol = 128 * len(tiles)
            bi = sgi % 2
            uT, gT, xs, xhb = uT2[bi], gT2[bi], xs2[bi], xhb2[bi]
            K = lambda *a: (tag, bi) + a if a[0] in ("uT", "xs", "xhb") else (tag,) + a
            for i, t in enumerate(tiles):
                s = t % 2
                P.dma("ssx%d" % s, xt[s][:], D.X2[t * 128:(t + 1) * 128, :], reads=[("dram", "X2", t)], writes=[K("xt", s)])
                rms_to_hT(P, C, xt[s][:], K("xt", s), gbc[:], K("gbc"), hb[s], K("hb", s), hT[s][:], [K("hT", s)], pT, ss[s], K("ss", s), junk[:])
                for half in range(2):
                    for f4 in range(4):
                        fc = half * 4 + f4
                        for kc in range(8):
                            P.pe(M("matmul", pU[half][:, f4, :], lhsT=win[:, kc, fc * 128:(fc + 1) * 128], rhs=hT[s][:, kc, :], start=(kc == 0), stop=(kc == 7)),
                                 reads=[K("hT", s), K("win", kc)], writes=[K("pU", half)])
                    P.act(M("copy", out=uT[:, half * 4:half * 4 + 4, i * 128:(i + 1) * 128], in_=pU[half][:]), reads=[K("pU", half)], writes=[K("uT", i)])
            ukeys = [K("uT", i) for i in range(len(tiles))]
            for fc in range(8):
                for ri in range(2):
                    for tp in range(8):
                        for q4 in range(4):
                            P.pe(M("matmul", pSs[q4][:, ri * 64:ri * 64 + nch], lhsT=Wp2[32 * q4:32 * q4 + 32, fc, 7 - tp, ri].rearrange("p a c -> p (a c)"),
                                   rhs=uT[32 * q4:32 * q4 + 32, fc, tp:tp + 8 * (nch - 1) + 1:8], start=(tp == 0), stop=(tp == 7), tile_position=(32 * q4, 0)),
                                 reads=ukeys + [K("Wp2")], writes=[K("pSs", q4)])
                for q4 in range(4):
                    pr = 4 * fc + q4
                    src = pSs[q4][:, 0:128].rearrange("p (r j) -> p r j", r=2)[:, :, 0:nch]
                    if sample:
                        P.act(M("copy", out=xs_s[:, :, pr, 1, :], in_=src), reads=[K("pSs", q4)], writes=[K("xs_s", 1)])
                    else:
                        P.act(M("copy", out=xs[:, :, pr, 1:nch + 1], in_=src), reads=[K("pSs", q4)], writes=[K("xs")])
            if sample:
                X = xs_s[:, :, :, 0, :]
                S_ = xs_s[:, :, :, 1, :]
                bA = lambda a: a.unsqueeze(3).broadcast_to([128, 2, 32, 4])
                r1s = xhb_s
                dv("tensor_copy", out=xhb_s[:], in_=X, reads=[K("xs_s", 0)], writes=[K("xhb")])
                t1 = g1[:, 0:256].rearrange("p (r q b) -> p r q b", r=2, q=32)
                t2 = g2[:, 0:256].rearrange("p (r q b) -> p r q b", r=2, q=32)
                dv("tensor_tensor", out=t1, in0=X, in1=bA(Aa[:]), op=ALU.mult, reads=[K("xs_s", 0), K("Aa")], writes=[K("g1")])
                dv("tensor_tensor", out=t2[:, 0], in0=X[:, 1], in1=bA(Ab[:])[:, 0], op=ALU.mult, reads=[K("xs_s", 0), K("Ab")], writes=[K("g2")])
                dv("tensor_tensor", out=t2[:, 1], in0=X[:, 0], in1=bA(Ab[:])[:, 1], op=ALU.mult, reads=[K("xs_s", 0), K("Ab")], writes=[K("g2")])
                dv("tensor_tensor", out=t1, in0=t1, in1=t2, op=ALU.add, reads=[K("g1"), K("g2")], writes=[K("g1")])
                dv("tensor_tensor", out=S_, in0=S_, in1=t1, op=ALU.add, reads=[K("g1"), K("xs_s", 1)], writes=[K("xs_s", 1)])
                P.dma("ssosts", D.sts, xs_s[:, :, :, 1, :], reads=[K("xs_s", 1)])
                xin = xhb_s
            else:
                for j in range(nch):
                    X = xs[:, :, :, j]
                    S_ = xs[:, :, :, j + 1]
                    dv("tensor_tensor", out=r1[:], in0=X, in1=Aa[:], op=ALU.mult, reads=[K("xs"), K("Aa")], writes=[K("r1")])
                    dv("tensor_tensor", out=r2[:, 0], in0=X[:, 1], in1=Ab[:, 0], op=ALU.mult, reads=[K("xs"), K("Ab")], writes=[K("r2")])
                    dv("tensor_tensor", out=r2[:, 1], in0=X[:, 0], in1=Ab[:, 1], op=ALU.mult, reads=[K("xs"), K("Ab")], writes=[K("r2")])
                    dv("tensor_tensor", out=r1[:], in0=r1[:], in1=r2[:], op=ALU.add, reads=[K("r1"), K("r2")], writes=[K("r1")])
                    dv("tensor_tensor", out=S_, in0=S_, in1=r1[:], op=ALU.add, reads=[K("r1"), K("xs")], writes=[K("xs")])
                P.act(M("copy", out=xhb[:], in_=xs[:, :, :, 0:nch]), reads=[K("xs")], writes=[K("xhb")])
                xin = xhb
            for fc in range(8):
                for t in range(8):
                    o = pY[:, t * 64:t * 64 + nch]
                    for tp in range(t + 1):
                        P.pe(M("matmul", o, lhsT=Kblk[:, fc, t - tp, :], rhs=uT[:, fc, tp:tp + 8 * (nch - 1) + 1:8], start=(tp == 0), stop=False),
                             reads=ukeys + [K("Kblk", fc)], writes=[K("pY")])
                    for ri in range(2):
                        for q4 in range(4):
                            pr = 4 * fc + q4
                            P.pe(M("matmul", pY[32 * q4:32 * q4 + 32, t * 64:t * 64 + nch], lhsT=Vp[:, pr, t + 1, ri].rearrange("p a c -> p (a c)"),
                                   rhs=xin[:, ri, pr, 0:nch], start=False, stop=(q4 == 3 and ri == 1), tile_position=(0, 32 * q4)),
                                 reads=[K("Vp"), K("xhb")], writes=[K("pY")])
                yv = pY[:].rearrange("p (t j) -> p t j", t=8)[:, :, 0:nch]
                v3 = lambda ap: ap[:, 0:8 * nch].rearrange("p (t j) -> p t j", t=8)
                P.act(M("activation", out=v3(g1[:]), in_=yv, func=AF.Square), reads=[K("pY")], writes=[K("g1")])
                dv("tensor_scalar", out=v3(g1[:]), in0=v3(g1[:]), scalar1=0.044715, scalar2=1.0, op0=ALU.mult, op1=ALU.add, reads=[K("g1")], writes=[K("g1")])
                dv("tensor_tensor", out=v3(g2[:]), in0=v3(g1[:]), in1=yv, op=ALU.mult, reads=[K("g1"), K("pY")], writes=[K("g2")])
                P.act(M("activation", out=v3(g1[:]), in_=v3(g2[:]), func=AF.Sigmoid, scale=1.5957691216057308), reads=[K("g2")], writes=[K("g1")])
                gdst = gT[:, fc, 0:8 * nch].rearrange("p (j t) -> p t j", t=8)
                dv("tensor_tensor", out=gdst, in0=v3(g1[:]), in1=yv, op=ALU.mult, reads=[K("g1"), K("pY")], writes=[K("gT", fc)])
            gkeys = [K("gT", fc) for fc in range(8)]
            c0 = tiles[0] * 128
            P.dma("ssgst", D.GT[:, :, c0:c0 + ncol].rearrange("f p t -> p f t"), gT[:, :, 0:ncol], reads=gkeys,
                  writes=[("dram", "GT", t) for t in tiles])

        nseg = 32 // SEG_T
        for sg_ in range(nseg):
            segment(list(range(sg_ * SEG_T, (sg_ + 1) * SEG_T)), NCH, False, sg_)
            if sg_ < nseg - 1:
                P.dve(M("tensor_copy", out=xs2[(sg_ + 1) % 2][:, :, :, 0:1], in_=xs2[sg_ % 2][:, :, :, NCH:NCH + 1]),
                      reads=[(tag, sg_ % 2, "xs")], writes=[(tag, (sg_ + 1) % 2, "xs")])
        lastb = (nseg - 1) % 2
        P.dve(M("tensor_copy", out=r2[:], in_=xs2[lastb][:, :, :, NCH]), reads=[(tag, lastb, "xs")], writes=[K("r2")])
        P.dma("ssostp", D.stp, r2[:], reads=[K("r2")])
        segment([32], 4, True, nseg)
    P.barrier(C.dummy[:])
    glu_phase(P, C, nc, D)


def glu_phase(P, C, nc, D):
    tag = "gl"
    K = lambda *a: (tag,) + a
    with contextlib.ExitStack() as st:
        sb = lambda name, shape, dt: st.enter_context(nc.sbuf_tensor(tag + name, shape, dt))
        ps = lambda name, shape, dt: st.enter_context(nc.psum_tensor(tag + name, shape, dt))
        wglu = sb("wglu", [128, 8, 2 * DM], BF16)
        load_w_bf16(P, "glw", wglu, D.ssm_w_glu.rearrange("(kc p) n -> p kc n", p=128), 8, K("wglu"))
        gt = [sb("gt%d" % i, [128, 8, 128], BF16) for i in range(3)]
        xt = [sb("xt%d" % i, [128, DM], F32) for i in range(3)]
        sg = [sb("sg%d" % i, [128, 512], F32) for i in range(4)]
        pV = [ps("pV%d" % i, [128, 512], F32) for i in range(4)]
        pG = [ps("pG%d" % i, [128, 512], F32) for i in range(4)]
        for t in range(NT):
            s = t % 3
            P.dma("glg%d" % s, gt[s][:], D.GT[:, :, t * 128:(t + 1) * 128].rearrange("f p t -> p f t"), reads=[("dram", "GT", t)], writes=[K("gt", s)])
            P.dma("glx%d" % s, xt[s][:], D.X2[t * 128:(t + 1) * 128, :], reads=[("dram", "X2", t)], writes=[K("xt", s)])
            for half in range(2):
                q = (2 * t + half) % 4
                for vg, pp in ((0, pV), (1, pG)):
                    nb = vg * 2 + half
                    for fc in range(8):
                        P.pe(M("matmul", pp[q][:], lhsT=gt[s][:, fc, :], rhs=wglu[:, fc, nb * 512:(nb + 1) * 512], start=(fc == 0), stop=(fc == 7)),
                             reads=[K("gt", s), K("wglu", fc)], writes=[K("p", vg, q)])
                P.act(M("activation", out=sg[q][:], in_=pG[q][:], func=AF.Sigmoid), reads=[K("p", 1, q)], writes=[K("sg", q)])
                P.dve(M("tensor_tensor", out=sg[q][:], in0=sg[q][:], in1=pV[q][:], op=ALU.mult), reads=[K("sg", q), K("p", 0, q)], writes=[K("sg", q)])
                P.pool(M("tensor_tensor", out=xt[s][:, half * 512:(half + 1) * 512], in0=xt[s][:, half * 512:(half + 1) * 512], in1=sg[q][:], op=ALU.add),
                       reads=[K("sg", q), K("xt", s)], writes=[K("xt", s)])
            P.dma("glst%d" % s, D.X3[t * 128:(t + 1) * 128, :], xt[s][:], reads=[K("xt", s)], writes=[("dram", "X3", t)])


def build(stage=9):
    nc = bass.Bass("TRN2", target_bir_lowering=False)
    D = Ctx()
    C = Ctx()
    din = lambda name, shape, dt=F32: nc.dram_tensor(name, shape, dt, kind="ExternalInput").ap()
    dout = lambda name, shape, dt=F32: nc.dram_tensor(name, shape, dt, kind="ExternalOutput").ap()
    dscr = lambda name, shape, dt=F32: nc.dram_tensor(name, shape, dt, kind="Internal").ap()
    D.xp = din("xp", [4096, DM])
    D.xs = din("xs", [32, DM])
    D.rope = din("rope", [NT, 128, 2, 32])
    ident = din("ident", [128, 128])
    D.mask_p = din("mask_p", [128, 512])
    D.mask_s = din("mask_s", [128, 13, 8])
    D.mask_n = din("mask_n", [8, 3, 8])
    D.cache = [din("cache%d" % g, [4, win, 2, 512]) for g, (win, dil) in enumerate(GROUPS)]
    norm_mix = din("norm_mix", [2, DM])
    norm_ffn = din("norm_ffn", [2, DM])
    D.norm_mix0 = norm_mix[0]
    D.w_qkv = din("w_qkv", [DM, 4608])
    D.q_norm = din("q_norm", [2, 32])
    D.k_norm = din("k_norm", [2, 32])
    D.w_o = din("w_o", [512, DM])
    wg = din("ffn_w_gate", [2, DM, DFF])
    wu = din("ffn_w_up", [2, DM, DFF])
    wd = din("ffn_w_down", [2, DFF, DM])
    D.norm_mix1 = norm_mix[1]
    D.ident = ident
    D.ssm_w_in = din("ssm_w_in", [DM, DM])
    D.ssm_w_glu = din("ssm_w_glu", [DM, 2 * DM])
    for nm in ("are", "aim", "ldt", "bre", "bim"):
        setattr(D, "s_%s_B" % nm, din("s_%s_B" % nm, [128, 512]))
    for nm in ("are", "aim", "ldt"):
        setattr(D, "s_%s_C" % nm, din("s_%s_C" % nm, [128, 32]))
    for nm in ("bre", "bim", "cre", "cim"):
        setattr(D, "s_%s_C" % nm, din("s_%s_C" % nm, [128, 32, 16]))
    D.s_d = din("s_d", [128, 8])
    D.s_pm = din("s_pm", [128, 2])
    D.s_qm = din("s_qm", [128, 2])
    D.st0 = din("st0", [128, 2, 32, 4])
    D.stp = dout("stp", [128, 2, 32])
    D.sts = dout("sts", [128, 2, 32, 4])
    D.yp = dout("yp", [4096, DM])
    D.ys = dout("ys", [32, DM])
    D.kvp = [dout("kvp%d" % g, [win, 2, 512]) for g, (win, dil) in enumerate(GROUPS)]
    D.kvs = [dout("kvs%d" % g, [4, win, 2, 512]) for g, (win, dil) in enumerate(GROUPS)]
    D.dbg_OT = dout("dbg_OT", [64, 8, TOK]) if stage < 9 else None
    D.dbg_QKT = dout("dbg_QKT", [24, 128, TOK], BF16) if stage < 9 else None
    D.dbg_Vs = dout("dbg_Vs", [TOK, 1536], BF16) if stage < 9 else None
    D.QKT = dscr("QKT", [24, 128, TOK], BF16)
    D.Vs = dscr("Vs", [TOK, 1536], BF16)
    D.X1 = dscr("X1", [TOK, DM])
    D.X2 = dscr("X2", [TOK, DM])
    D.X3 = dscr("X3", [TOK, DM])
    D.GT = dscr("GT", [8, 128, TOK], BF16)
    P = Prog(nc)
    with contextlib.ExitStack() as st:
        C.idb = st.enter_context(nc.sbuf_tensor("idb", [128, 128], BF16))
        C.dummy = st.enter_context(nc.sbuf_tensor("bar_dummy", [128, 8], F32))
        P.dma("id", C.idb[:], ident, writes=["idb"], eng="pool")
        attn_a1(P, C, nc, D)
        if D.dbg_QKT is not None:
            for f in range(24):
                P.dma("dbgq", D.dbg_QKT[f], D.QKT[f], reads=[("dram", "QKT", t) for t in range(NT)])
            for t in range(NT):
                P.dma("dbgv", D.dbg_Vs[t * 128:(t + 1) * 128, :], D.Vs[t * 128:(t + 1) * 128, :], reads=[("dram", "Vs", t)])
        P.barrier(C.dummy[:])
        attn_a2(P, C, nc, D)
        P.barrier(C.dummy[:])
        if stage == 1:
            ffn_phase(P, C, nc, D.X1, "X1", None, None, wg[0], wu[0], wd[0], norm_ffn[0], "f0", final_out=(D.yp, D.ys))
        else:
            ffn_phase(P, C, nc, D.X1, "X1", D.X2, "X2", wg[0], wu[0], wd[0], norm_ffn[0], "f0")
            P.barrier(C.dummy[:])
            ssm_phase(P, C, nc, D)
            P.barrier(C.dummy[:])
            if stage == 2:
                for t in range(32):
                    P.dma("dbgx3", D.yp[t * 128:(t + 1) * 128, :], D.X3[t * 128:(t + 1) * 128, :], reads=[("dram", "X3", t)])
                P.dma("dbgx3", D.ys[:, :], D.X3[4096:4128, :], reads=[("dram", "X3", 32)])
            else:
                ffn_phase(P, C, nc, D.X3, "X3", None, None, wg[1], wu[1], wd[1], norm_ffn[1], "f1", final_out=(D.yp, D.ys))
        P.emit()
    return nc


def host_consts():
    half = 32
    inv = (10000.0 ** (-np.arange(half, dtype=np.float32) / half)).astype(np.float32)
    pos = np.zeros((NT, 128), np.float32)
    pos[:32] = np.arange(4096, dtype=np.float32).reshape(32, 128)
    pos[32] = 16384 + (np.arange(128) % 8)
    ang = pos[:, :, None] * inv[None, None, :]
    rope = np.stack([np.cos(ang), np.sin(ang)], axis=2).astype(np.float32)
    n = np.arange(128)[:, None]
    m = np.arange(128)[None, :]
    mask_p = np.concatenate([(n <= m), (n >= m), (n <= m), (n >= m)], axis=1).astype(np.float32)
    mask_s = np.zeros((128, 13, 8), np.float32)
    i = np.arange(8)[None, :]
    mask_s[:, 0, :] = (n >= i)
    for k in range(4):
        row = 128 * k + n
        mask_s[:, 1 + k, :] = (row % 4 == i % 4) & (row >= i)
    for r in range(8):
        mask_s[:, 5 + r, :] = (r == i)
    mask_n = np.zeros((8, 3, 8), np.float32)
    kn = np.arange(8)[:, None]
    for g, (win, dil) in enumerate(GROUPS):
        mask_n[:, g, :] = (kn <= i) & ((i - kn) % dil == 0)
    return dict(rope=rope, ident=np.eye(128, dtype=np.float32), mask_p=mask_p, mask_s=mask_s, mask_n=mask_n)


def ssm_layouts(inp, c):
    f = lambda a: np.ascontiguousarray(a, dtype=np.float32)
    a_re, a_im, ldt = inp["ssm_a_re"][0], inp["ssm_a_im"][0], inp["ssm_log_dt"][0]
    b_re, b_im = inp["ssm_b_re"][0], inp["ssm_b_im"][0]
    c_re, c_im = inp["ssm_c_re"][0], inp["ssm_c_im"][0]
    out = {}
    def lb_gp(a):
        x = a.reshape(8, 8, 64)
        x = np.broadcast_to(x[:, :, None, :], (8, 8, 16, 64))
        return f(x.transpose(1, 2, 0, 3).reshape(128, 512))
    out["s_are_B"] = lb_gp(a_re)
    out["s_aim_B"] = lb_gp(a_im)
    out["s_ldt_B"] = lb_gp(np.broadcast_to(ldt[:, None], (64, 64)))
    lb_b = lambda b: f(b.reshape(8, 8, 64, 16).transpose(1, 3, 0, 2).reshape(128, 512))
    out["s_bre_B"] = lb_b(b_re)
    out["s_bim_B"] = lb_b(b_im)
    lc_gp = lambda a: f(a.reshape(32, 2, 64).transpose(1, 2, 0).reshape(128, 32))
    out["s_are_C"] = lc_gp(a_re)
    out["s_aim_C"] = lc_gp(a_im)
    out["s_ldt_C"] = lc_gp(np.broadcast_to(ldt[:, None], (64, 64)))
    lc_b = lambda b: f(b.reshape(32, 2, 64, 16).transpose(1, 2, 0, 3).reshape(128, 32, 16))
    out["s_bre_C"] = lc_b(b_re)
    out["s_bim_C"] = lc_b(b_im)
    lc_c = lambda cc: f(cc.reshape(32, 2, 16, 64).transpose(1, 3, 0, 2).reshape(128, 32, 16))
    out["s_cre_C"] = lc_c(c_re)
    out["s_cim_C"] = lc_c(c_im)
    out["s_d"] = f(inp["ssm_d"][0].reshape(8, 128).T)
    g8 = np.arange(128) // 16
    out["s_pm"] = f(np.stack([(g8 % 2 == 0), (g8 % 2 == 1)], axis=1))
    par = np.arange(128) // 64
    out["s_qm"] = f(np.stack([(par == 0), (par == 1)], axis=1))
    st = inp["state_ssm"][0, 4 * c:4 * c + 4]
    out["st0"] = f(st.reshape(4, 32, 2, 64, 2).transpose(2, 3, 4, 1, 0).reshape(128, 2, 32, 4))
    return out


def make_in_maps(inp):
    cst = host_consts()
    maps = []
    for c in range(8):
        m = dict(cst)
        m["xp"] = np.ascontiguousarray(inp["x_prompt"][c])
        m["xs"] = np.ascontiguousarray(inp["x_sample"][4 * c:4 * c + 4].reshape(32, DM))
        for g, nm in enumerate(("cache_kv_w128", "cache_kv_w512", "cache_kv_w2048")):
            a = inp[nm][0, 4 * c:4 * c + 4]
            m["cache%d" % g] = np.ascontiguousarray(a.reshape(4, a.shape[1], 2, 512))
        m["norm_mix"] = inp["norm_mix"]
        m["norm_ffn"] = inp["norm_ffn"]
        m["w_qkv"] = inp["w_qkv"][0]
        m["q_norm"] = inp["q_norm"].reshape(2, 32)
        m["k_norm"] = inp["k_norm"].reshape(2, 32)
        m["w_o"] = inp["w_o"][0]
        m["ffn_w_gate"] = inp["ffn_w_gate"]
        m["ffn_w_up"] = inp["ffn_w_up"]
        m["ffn_w_down"] = inp["ffn_w_down"]
        m["ssm_w_in"] = inp["ssm_w_in"][0]
        m["ssm_w_glu"] = inp["ssm_w_glu"][0]
        m.update(ssm_layouts(inp, c))
        maps.append(m)
    return maps


_NC_CACHE = {}


def kernel(**inputs):
    inp = {k: np.asarray(v) for k, v in inputs.items()}
    if "nc" not in _NC_CACHE:
        _NC_CACHE["nc"] = build(stage=9)
    nc = _NC_CACHE["nc"]
    maps = make_in_maps(inp)
    res = run_bass_kernel_spmd(nc, maps, core_ids=list(range(8))).results
    y_p = np.stack([r["yp"] for r in res], axis=0)
    y_s = np.concatenate([r["ys"].reshape(4, 8, DM) for r in res], axis=0)
    kvp = [np.stack([r["kvp%d" % g].reshape(win, 2, 8, 64) for r in res], axis=0)[None] for g, (win, dil) in enumerate(GROUPS)]
    kvs = [np.concatenate([r["kvs%d" % g].reshape(4, win, 2, 8, 64) for r in res], axis=0)[None] for g, (win, dil) in enumerate(GROUPS)]
    stp = np.stack([r["stp"].reshape(2, 64, 2, 32).transpose(3, 0, 1, 2).reshape(64, 64, 2) for r in res], axis=0)[None]
    sts = np.concatenate([r["sts"].reshape(2, 64, 2, 32, 4).transpose(4, 3, 0, 1, 2).reshape(4, 64, 64, 2) for r in res], axis=0)[None]
    f = lambda a: np.ascontiguousarray(a, dtype=np.float32)
    return (f(y_p), f(y_s), f(kvp[0]), f(kvp[1]), f(kvp[2]), f(stp), f(kvs[0]), f(kvs[1]), f(kvs[2]), f(sts))
```

```python
import numpy as np
import concourse.bass as bass
import concourse.mybir as mybir
from concourse.bass_utils import run_bass_kernel_spmd

F32 = mybir.dt.float32
BF16 = mybir.dt.bfloat16
AF = mybir.ActivationFunctionType
ALU = mybir.AluOpType
AX = mybir.AxisListType


def M(name, *a, **k):
    return (name, a, k)


def _mkfn(spec):
    name, a, k = spec
    return lambda e: getattr(e, name)(*a, **k)


class _Op:
    __slots__ = ("eng", "fn", "reads", "writes", "dma", "signal", "count", "waits", "idx", "_deps", "is_barrier", "spec", "alld", "rawd", "phase", "t_end", "pos", "_dma_all", "pdma")

    def __init__(self, eng, fn, reads, writes, dma):
        self.eng = eng
        self.fn = _mkfn(fn) if isinstance(fn, tuple) else fn
        self.spec = fn if isinstance(fn, tuple) else None
        self.reads = tuple(reads)
        self.writes = tuple(writes)
        self.dma = dma
        self.signal = False
        self.is_barrier = False
        self.count = 0
        self.waits = []


class Prog:
    ENGS = ("pe", "act", "dve", "pool", "sp")

    def __init__(self, nc):
        self.nc = nc
        self.ops = []

    def add(self, eng, fn, reads=(), writes=(), dma=None):
        op = _Op(eng, fn, reads, writes, dma)
        op.idx = len(self.ops)
        self.ops.append(op)
        return op

    def pe(self, fn, reads=(), writes=()):
        return self.add("pe", fn, reads, writes)

    def act(self, fn, reads=(), writes=()):
        return self.add("act", fn, reads, writes)

    def dve(self, fn, reads=(), writes=()):
        return self.add("dve", fn, reads, writes)

    def pool(self, fn, reads=(), writes=()):
        return self.add("pool", fn, reads, writes)

    def dma(self, key, out, in_, reads=(), writes=(), eng="sp", **kw):
        return self.add(eng, M("dma_start", out=out, in_=in_, **kw), reads, writes, dma=key)

    def barrier(self, dummy):
        op = _Op("pool", M("memset", dummy, 0.0), (), (), None)
        op.idx = len(self.ops)
        op.is_barrier = True
        self.ops.append(op)
        return op

    @staticmethod
    def _cost(op):
        spec = op.spec
        if spec is None:
            return 0.2
        name, a, k = spec
        def fsz(ap):
            n = 1
            for d in ap.shape[1:]:
                n *= d
            return n
        if op.dma is not None:
            return {"sp": 0.06, "act": 0.06, "pool": 1.0}.get(op.eng, 0.1)
        if name in ("matmul", "transpose"):
            mv = k.get("rhs") if name == "matmul" else k.get("in_")
            n = fsz(mv) if name == "matmul" else 128
            f = 4.0 if (name == "matmul" and k.get("lhsT").dtype == F32) else 1.0
            return 0.02 + f * max(64, n) / 2000.0
        out = k.get("out") if "out" in k else (a[0] if a else None)
        n = fsz(out) if out is not None else 64
        if op.eng == "act":
            return 0.22 + n / 1200.0
        if op.eng == "pool":
            return 0.25 + n / 600.0
        return 0.16 + n / 960.0

    @staticmethod
    def _dma_time(op):
        name, a, k = op.spec
        out = k["out"]
        n = 1
        for d in out.shape:
            n *= d
        nbytes = n * (2 if out.dtype == BF16 else 4)
        return 2.0 + nbytes / 60000.0

    def finalize(self):
        import heapq
        ops = self.ops
        last_w, readers = {}, {}
        phase = 0
        for op in ops:
            op.phase = phase
            if op.is_barrier:
                phase += 1
                op.alld, op.rawd = set(), set()
                continue
            alld, rawd = set(), set()
            for r in op.reads:
                w = last_w.get(r)
                if w is not None:
                    alld.add(w)
                    rawd.add(w)
            for wk in op.writes:
                w = last_w.get(wk)
                if w is not None:
                    alld.add(w)
                for rd in readers.get(wk, ()):
                    alld.add(rd)
            alld.discard(op)
            rawd.discard(op)
            op.alld = set(d for d in alld if d.phase == op.phase)
            op.rawd = rawd
            for r in op.reads:
                readers.setdefault(r, []).append(op)
            for wk in op.writes:
                last_w[wk] = op
                readers[wk] = []
        nphase = phase + 1
        by_phase = [[] for _ in range(nphase)]
        barriers = {}
        for op in ops:
            if op.is_barrier:
                barriers[op.phase] = op
            else:
                by_phase[op.phase].append(op)
        order = {e: [] for e in self.ENGS}
        tnow = 0.0
        for ph in range(nphase):
            pops = by_phase[ph]
            nun = {}
            users = {}
            for op in pops:
                nun[op] = len(op.alld)
                for d in op.alld:
                    users.setdefault(d, []).append(op)
            efree = {e: tnow for e in self.ENGS}
            wait_h = {e: [] for e in self.ENGS}
            avail_h = {e: [] for e in self.ENGS}
            ready_t = {}
            for op in pops:
                if nun[op] == 0:
                    heapq.heappush(wait_h[op.eng], (tnow, op.idx, op))
            done = 0
            tend = tnow
            while done < len(pops):
                best = None
                for e in self.ENGS:
                    wh, ah = wait_h[e], avail_h[e]
                    while wh and wh[0][0] <= efree[e]:
                        t_, i_, o_ = heapq.heappop(wh)
                        heapq.heappush(ah, (i_, o_))
                    if ah:
                        cand = (efree[e], ah[0][0], e, 0)
                    elif wh:
                        cand = (wh[0][0], wh[0][1], e, 1)
                    else:
                        continue
                    if best is None or cand < best:
                        best = cand
                assert best is not None, "scheduler deadlock"
                st_, _, e, src = best
                if src == 0:
                    _, op = heapq.heappop(avail_h[e])
                else:
                    _, _, op = heapq.heappop(wait_h[e])
                dur = self._cost(op)
                fin = st_ + dur
                efree[e] = fin
                comp = fin + (self._dma_time(op) if op.dma is not None else 0.0)
                op.t_end = comp
                tend = max(tend, comp)
                order[e].append(op)
                done += 1
                for u in users.get(op, ()):
                    nun[u] -= 1
                    rt = max(ready_t.get(u, tnow), comp + (0.25 if (u.eng != e or op.dma is not None) else 0.0))
                    ready_t[u] = rt
                    if nun[u] == 0:
                        heapq.heappush(wait_h[u.eng], (rt, u.idx, u))
            tnow = tend + 1.0
            if ph in barriers:
                order["pool"].append(barriers[ph])
        self.order = order
        self.est_us = tnow
        for e in self.ENGS:
            for i, op in enumerate(order[e]):
                op.pos = i
        seq = []
        for e in self.ENGS:
            seq.extend(order[e])
        last_eng = {}
        last_dma = {}
        cur_barrier = {}
        for op in ops:
            op._deps = []
        for ph in range(nphase):
            pass
        eng_last_by_phase = {}
        for e in self.ENGS:
            for op in order[e]:
                if not op.is_barrier and op.dma is None:
                    eng_last_by_phase[(e, op.phase)] = op
        per_phase_idx = {}
        per_phase_eng = {}
        for e in self.ENGS:
            for op in order[e]:
                if op.dma is not None:
                    d = per_phase_idx.setdefault(op.phase, {})
                    if op.dma not in d:
                        d[op.dma] = len(d)
                    op.pdma = d[op.dma]
                    assert per_phase_eng.setdefault((op.phase, op.dma), e) == e, "dma key %s used from two queues" % (op.dma,)
        for op in ops:
            if op.is_barrier:
                deps = []
                for e in self.ENGS:
                    for ph in range(op.phase, -1, -1):
                        if (e, ph) in eng_last_by_phase:
                            deps.append(eng_last_by_phase[(e, ph)])
                            break
                op._deps = deps
                op._dma_all = True
            else:
                keep = []
                newest = {}
                for d in op.alld:
                    if d.dma is not None or op.dma is not None:
                        keep.append(d)
                        continue
                    if d.eng == op.eng:
                        if op.eng == "pe" or d not in op.rawd:
                            continue
                    if d.eng not in newest or newest[d.eng].pos < d.pos:
                        newest[d.eng] = d
                keep.extend(newest.values())
                if op.phase > 0:
                    keep.append(barriers[op.phase - 1])
                op._deps = keep
                op._dma_all = False
        for op in ops:
            for d in op._deps:
                if d.dma is None:
                    d.signal = True
        eng_cnt = {e: 0 for e in self.ENGS}
        dma_cnt = {}
        dma_cnt_at_barrier = {}
        for e in self.ENGS:
            for op in order[e]:
                if op.dma is None and op.signal:
                    eng_cnt[e] += 1
                    op.count = eng_cnt[e]
        for ph in range(nphase):
            for e in self.ENGS:
                for op in order[e]:
                    if op.dma is not None and op.phase == ph:
                        dma_cnt[op.pdma] = dma_cnt.get(op.pdma, 0) + 16
                        op.count = dma_cnt[op.pdma]
            dma_cnt_at_barrier[ph] = dict(dma_cnt)
        waited = {e: {} for e in self.ENGS}
        for e in self.ENGS:
            wl = waited[e]
            for op in order[e]:
                need = {}
                for d in op._deps:
                    k = ("dma", d.pdma) if d.dma is not None else ("eng", d.eng)
                    need[k] = max(need.get(k, 0), d.count)
                if op._dma_all:
                    for k, v in dma_cnt_at_barrier[op.phase].items():
                        need[("dma", k)] = max(need.get(("dma", k), 0), v)
                for k, v in need.items():
                    if wl.get(k, 0) >= v:
                        continue
                    wl[k] = v
                    op.waits.append((k, v))
        self.dma_keys = list(dma_cnt.keys())
        self.dma_final = dma_cnt
        self.eng_final = eng_cnt

    def emit(self):
        nc = self.nc
        self.finalize()
        import contextlib
        with contextlib.ExitStack() as st:
            sems = {}
            for e in self.ENGS:
                sems[("eng", e)] = st.enter_context(nc.semaphore("s_" + e))
            for i, k in enumerate(self.dma_keys):
                sems[("dma", k)] = st.enter_context(nc.semaphore("d%d" % i))
            print("n_sems", len(sems), "n_ops", len(self.ops),
                  {e: sum(1 for o in self.ops if o.eng == e) for e in self.ENGS})
            block = st.enter_context(nc.Block())
            per = self.order
            print("scheduler estimate: %.1f us" % self.est_us)

            def run(eng, name):
                for op in per[name]:
                    for k, v in op.waits:
                        eng.wait_ge(sems[k], v)
                    ins = op.fn(eng)
                    if op.dma is not None:
                        ins.then_inc(sems[("dma", op.pdma)], 16)
                    elif op.signal:
                        ins.then_inc(sems[("eng", name)], 1)
                if name == "sp":
                    for k, v in self.dma_final.items():
                        eng.wait_ge(sems[("dma", k)], v)
                    for e, v in self.eng_final.items():
                        if v > 0:
                            eng.wait_ge(sems[("eng", e)], v)

            @block.tensor
            def _(e):
                run(e, "pe")

            @block.scalar
            def _(e):
                run(e, "act")

            @block.vector
            def _(e):
                run(e, "dve")

            @block.gpsimd
            def _(e):
                run(e, "pool")

            @block.sync
            def _(e):
                run(e, "sp")


import contextlib
import ml_dtypes

NT = 33
TOK = NT * 128
DM = 1024
DFF = 2816
NFC = 22
GROUPS = ((128, 1), (512, 4), (2048, 16))
EPS = 1e-6


class Ctx:
    pass


def rms_to_hT(P, C, xt, xkey, gbc, gkey, hb, hbkey, hT_dst, hTkeys, pT, ss, sskey, junk):
    P.act(M("activation", out=junk, in_=xt, func=AF.Square), reads=[xkey], writes=["junk"])
    P.dve(M("tensor_reduce", out=ss[:, 0:1], in_=junk, axis=AX.X, op=ALU.add), reads=["junk"], writes=[(sskey, 0)])
    P.dve(M("tensor_scalar", out=ss[:, 1:2], in0=ss[:, 0:1], scalar1=1.0 / DM, scalar2=EPS,
                                    op0=ALU.mult, op1=ALU.add), reads=[(sskey, 0)], writes=[(sskey, 1)])
    P.act(M("activation", out=ss[:, 2:3], in_=ss[:, 1:2], func=AF.Sqrt), reads=[(sskey, 1)], writes=[(sskey, 2)])
    P.dve(M("reciprocal", out=ss[:, 3:4], in_=ss[:, 2:3]), reads=[(sskey, 2)], writes=[(sskey, 3)])
    P.dve(M("scalar_tensor_tensor", out=hb[:], in0=xt, scalar=ss[:, 3:4], in1=gbc, op0=ALU.mult, op1=ALU.mult),
          reads=[xkey, (sskey, 3), gkey], writes=[hbkey])
    for kc in range(8):
        P.pe(M("transpose", out=pT[:, kc, :], in_=hb[:, kc * 128:(kc + 1) * 128], identity=C.idb[:]),
             reads=[hbkey, "idb"], writes=["pT"])
    P.act(M("copy", out=hT_dst, in_=pT[:]), reads=["pT"], writes=hTkeys)


def load_w_bf16(P, key, dst, src_view, nchunk, wkey):
    for c in range(nchunk):
        P.dma(key, dst[:, c, :], src_view[:, c, :], writes=([wkey + (c_,) for c_ in range(nchunk)] if c == nchunk - 1 else []), eng="pool", max_dma_last_dim=4096)


def load_w_groups(P, key, dst, src_view, groups, wkey, axis):
    for gi, (lo, hi) in enumerate(groups):
        if axis == 2:
            P.dma("%s%d" % (key, gi), dst[:, :, lo:hi], src_view[:, :, lo:hi], writes=[wkey + (gi,)], eng="pool", max_dma_last_dim=4096)
        else:
            P.dma("%s%d" % (key, gi), dst[:, lo:hi, :], src_view[:, lo:hi, :], writes=[wkey + (gi,)], eng="pool", max_dma_last_dim=4096)


def ffn_phase(P, C, nc, src, skey, dst, dkey, wg, wu, wd, gvec, tag, final_out=None):
    with contextlib.ExitStack() as st:
        sb = lambda name, shape, dt: st.enter_context(nc.sbuf_tensor(tag + name, shape, dt))
        ps = lambda name, shape, dt: st.enter_context(nc.psum_tensor(tag + name, shape, dt))
        wgs = sb("wg", [128, 8, DFF], BF16)
        wus = sb("wu", [128, 8, DFF], BF16)
        wds = sb("wd", [128, NFC, DM], BF16)
        gbc = sb("gbc", [128, DM], F32)
        xt = [sb("xt%d" % i, [128, DM], F32) for i in range(2)]
        junk = sb("junk", [128, DM], F32)
        ss = [sb("ss%d" % i, [128, 4], F32) for i in range(2)]
        hb = [sb("hb%d" % i, [128, DM], BF16) for i in range(2)]
        hT = [sb("hT%d" % i, [128, 8, 256], BF16) for i in range(2)]
        aT = [sb("aT%d" % i, [128, NFC, 256], BF16) for i in range(2)]
        sg = [sb("sg%d" % i, [128, 256], BF16) for i in range(2)]
        xr = [sb("xr%d" % i, [128, DM], F32) for i in range(2)]
        pT = ps("pT", [128, 8, 128], BF16)
        pG = [ps("pG%d" % i, [128, 512], F32) for i in range(2)]
        pU = [ps("pU%d" % i, [128, 512], F32) for i in range(2)]
        pD = [ps("pD%d" % i, [128, 512], F32) for i in range(2)]
        K = lambda *a: (tag,) + a
        cg = [(c0, min(c0 + 512, DFF)) for c0 in range(0, DFF, 512)]
        fg = [(f0, min(f0 + 4, NFC)) for f0 in range(0, NFC, 4)]
        wgv = wg.rearrange("(kc p) n -> p kc n", p=128)
        wuv = wu.rearrange("(kc p) n -> p kc n", p=128)
        for gi in range(len(cg)):
            load_w_groups(P, tag + "wg%d_" % gi, wgs, wgv, [cg[gi]], K("wg", gi), 2)
            load_w_groups(P, tag + "wu%d_" % gi, wus, wuv, [cg[gi]], K("wu", gi), 2)
        load_w_groups(P, tag + "wd", wds, wd.rearrange("(kc p) n -> p kc n", p=128), fg, K("wd"), 1)
        P.dma(tag + "g", gbc[:], gvec.partition_broadcast(128), writes=[K("gbc")])
        nmt = (NT + 1) // 2
        for mt in range(nmt):
            tiles = [t for t in (2 * mt, 2 * mt + 1) if t < NT]
            ms = mt % 2
            ncol = 128 * len(tiles)
            for i, t in enumerate(tiles):
                s = t % 2
                P.dma(tag + "x%d" % s, xt[s][:], src[t * 128:(t + 1) * 128, :], reads=[("dram", skey, t)],
                      writes=[K("xt", s)])
                rms_to_hT(P, C, xt[s][:], K("xt", s), gbc[:], K("gbc"), hb[s], K("hb", s),
                          hT[ms][:, :, i * 128:(i + 1) * 128], [K("hT", ms, i)], pT, ss[s], K("ss", s), junk[:])
            hkeys = [K("hT", ms, i) for i in range(len(tiles))]
            for fc in range(NFC):
                q = fc % 2
                for kc in range(8):
                    P.pe(M("matmul", pG[q][:, 0:ncol], lhsT=wgs[:, kc, fc * 128:(fc + 1) * 128],
                                                              rhs=hT[ms][:, kc, 0:ncol], start=(kc == 0), stop=(kc == 7)),
                         reads=hkeys + [K("wg", fc // 4, 0)], writes=[K("pG", q)])
                for kc in range(8):
                    P.pe(M("matmul", pU[q][:, 0:ncol], lhsT=wus[:, kc, fc * 128:(fc + 1) * 128],
                                                              rhs=hT[ms][:, kc, 0:ncol], start=(kc == 0), stop=(kc == 7)),
                         reads=hkeys + [K("wu", fc // 4, 0)], writes=[K("pU", q)])
                P.act(M("activation", out=sg[q][:, 0:ncol], in_=pG[q][:, 0:ncol], func=AF.Silu),
                      reads=[K("pG", q)], writes=[K("sg", q)])
                P.dve(M("tensor_tensor", out=aT[ms][:, fc, 0:ncol], in0=sg[q][:, 0:ncol], in1=pU[q][:, 0:ncol], op=ALU.mult),
                      reads=[K("sg", q), K("pU", q)], writes=[K("aT", ms, fc)])
            for i, t in enumerate(tiles):
                s = t % 2
                P.dma(tag + "xr%d" % s, xr[s][:], src[t * 128:(t + 1) * 128, :], reads=[("dram", skey, t)], writes=[K("xr", s)])
                for half in range(2):
                    for fc in range(NFC):
                        P.pe(M("matmul", pD[half][:], lhsT=aT[ms][:, fc, i * 128:(i + 1) * 128],
                                                                       rhs=wds[:, fc, half * 512:(half + 1) * 512],
                                                                       start=(fc == 0), stop=(fc == NFC - 1)),
                             reads=[K("aT", ms, fc), K("wd", fc // 4)], writes=[K("pD", half)])
                    P.dve(M("tensor_tensor", out=xr[s][:, half * 512:(half + 1) * 512],
                                                                   in0=xr[s][:, half * 512:(half + 1) * 512], in1=pD[half][:], op=ALU.add),
                          reads=[K("xr", s), K("pD", half)], writes=[K("xr", s)])
                if final_out is None:
                    P.dma(tag + "st%d" % s, dst[t * 128:(t + 1) * 128, :], xr[s][:], reads=[K("xr", s)], writes=[("dram", dkey, t)])
                else:
                    yp, ys = final_out
                    if t < 32:
                        P.dma(tag + "st%d" % s, yp[t * 128:(t + 1) * 128, :], xr[s][:], reads=[K("xr", s)])
                    else:
                        P.dma(tag + "st%d" % s, ys[:, :], xr[s][0:32, :], reads=[K("xr", s)])


def attn_a1(P, C, nc, D):
    tag = "a1"
    K = lambda *a: (tag,) + a
    with contextlib.ExitStack() as st:
        sb = lambda name, shape, dt: st.enter_context(nc.sbuf_tensor(tag + name, shape, dt))
        ps = lambda name, shape, dt: st.enter_context(nc.psum_tensor(tag + name, shape, dt))
        wq = sb("wq", [128, 8, 4608], BF16)
        gbc = sb("gbc", [128, DM], F32)
        gains = sb("gains", [128, 2, 2, 32], F32)
        xt = [sb("xt%d" % i, [128, DM], F32) for i in range(2)]
        junk = sb("junk", [128, 3072], F32)
        jn = sb("jn", [128, DM], F32)
        ss = [sb("ss%d" % i, [128, 4], F32) for i in range(2)]
        hb = [sb("hb%d" % i, [128, DM], BF16) for i in range(2)]
        hT = [sb("hT%d" % i, [128, 8, 128], BF16) for i in range(2)]
        qkf = sb("qkf", [128, 3072], F32)
        kr = [sb("kr%d" % i, [128, 3072], F32) for i in range(2)]
        vf = [sb("vf%d" % i, [128, 1536], F32) for i in range(2)]
        vb = [sb("vb%d" % i, [128, 1536], BF16) for i in range(2)]
        qkb = sb("qkb", [128, 3072], BF16)
        qkTs = [sb("qkTs%d" % i, [128, 24, 128], BF16) for i in range(2)]
        ssq = sb("ssq", [128, 4, 48], F32)
        cs = [sb("cs%d" % i, [128, 2, 32], F32) for i in range(2)]
        tabs = sb("tabs", [128, 2, 2, 2, 32], F32)
        ra = sb("ra", [128, 2, 24, 32], F32)
        rb = sb("rb", [128, 2, 24, 32], F32)
        ra2 = sb("ra2", [128, 2, 24, 32], F32)
        rb2 = sb("rb2", [128, 2, 24, 32], F32)
        pT = ps("pT", [128, 8, 128], BF16)
        pQ = [ps("pQ%d" % i, [128, 512], F32) for i in range(4)]
        pT2 = [ps("pT2%d" % i, [128, 8, 128], BF16) for i in range(2)]

        load_w_groups(P, "a1w", wq, D.w_qkv.rearrange("(kc p) n -> p kc n", p=128), [(nb * 512, (nb + 1) * 512) for nb in range(9)], K("wq"), 2)
        P.dma("a1g", gbc[:], D.norm_mix0.partition_broadcast(128), writes=[K("gbc")])
        P.dma("a1gq", gains[:, 0, :, :], D.q_norm.partition_broadcast(128), writes=[K("gains", 0)])
        P.dma("a1gk", gains[:, 1, :, :], D.k_norm.partition_broadcast(128), writes=[K("gains", 1)])
        for t in range(NT):
            s = t % 2
            if t == 32:
                P.pool(M("memset", xt[s][:], 0.0), writes=[K("xt", s)])
                P.dma("a1x%d" % s, xt[s][0:32, :], D.xs[:, :], writes=[K("xt", s)])
            else:
                P.dma("a1x%d" % s, xt[s][:], D.xp[t * 128:(t + 1) * 128, :], writes=[K("xt", s)])
            P.dma("a1cs%d" % s, cs[s][:], D.rope[t], writes=[K("cs", s)])
            rms_to_hT(P, C, xt[s][:], K("xt", s), gbc[:], K("gbc"), hb[s], K("hb", s),
                      hT[s][:], [K("hT", s)], pT, ss[s], K("ss", s), jn[:])
            for nb in range(9):
                pq = nb % 4
                for kc in range(8):
                    P.pe(M("matmul", pQ[pq][:], lhsT=hT[s][:, kc, :], rhs=wq[:, kc, nb * 512:(nb + 1) * 512],
                                                                   start=(kc == 0), stop=(kc == 7)),
                         reads=[K("hT", s), K("wq", nb)], writes=[K("pQ", pq)])
                if nb < 6:
                    P.act(M("copy", out=qkf[:, nb * 512:(nb + 1) * 512], in_=pQ[pq][:]),
                          reads=[K("pQ", pq)], writes=[K("qkf", nb)])
                    P.act(M("activation", out=junk[:, nb * 512:(nb + 1) * 512], in_=pQ[pq][:], func=AF.Square),
                          reads=[K("pQ", pq)], writes=[("junkq", nb)])
                else:
                    j = nb - 6
                    P.act(M("copy", out=vf[s][:, j * 512:(j + 1) * 512], in_=pQ[pq][:]),
                          reads=[K("pQ", pq)], writes=[K("vf", s, j)])
                    P.act(M("copy", out=vb[s][:, j * 512:(j + 1) * 512], in_=pQ[pq][:]),
                          reads=[K("pQ", pq)], writes=[K("vb", s, j)])
            qkeys = [K("qkf", nb) for nb in range(6)]
            P.dve(M("tensor_reduce", out=ssq[:, 0, :], in_=junk[:].rearrange("p (a b) -> p a b", b=64), axis=AX.X, op=ALU.add),
                  reads=[("junkq", nb) for nb in range(6)], writes=[K("ssq", 0)])
            P.dve(M("tensor_scalar", out=ssq[:, 1, :], in0=ssq[:, 0, :], scalar1=1.0 / 64, scalar2=EPS, op0=ALU.mult, op1=ALU.add),
                  reads=[K("ssq", 0)], writes=[K("ssq", 1)])
            P.act(M("activation", out=ssq[:, 2, :], in_=ssq[:, 1, :], func=AF.Sqrt), reads=[K("ssq", 1)], writes=[K("ssq", 2)])
            P.dve(M("reciprocal", out=ssq[:, 3, :], in_=ssq[:, 2, :]), reads=[K("ssq", 2)], writes=[K("ssq", 3)])
            P.dve(M("tensor_tensor", out=qkf[:].rearrange("p (a b) -> p a b", b=64), in0=qkf[:].rearrange("p (a b) -> p a b", b=64),
                                            in1=ssq[:, 3, :].unsqueeze(2).broadcast_to([128, 48, 64]), op=ALU.mult),
                  reads=qkeys + [K("ssq", 3)], writes=qkeys + [K("qn")])
            for ci in range(2):
                P.pool(M("tensor_tensor", out=tabs[:, ci], in0=gains[:],
                                                             in1=cs[s][:, ci, :].unsqueeze(1).unsqueeze(1).broadcast_to([128, 2, 2, 32]), op=ALU.mult),
                       reads=[K("gains", 0), K("gains", 1), K("cs", s)], writes=[K("tabs", ci)])
            xv = qkf[:].rearrange("p (a g h d) -> p a g h d", a=2, g=24, h=2)
            kv_ = kr[s][:].rearrange("p (a g h d) -> p a g h d", a=2, g=24, h=2)
            x1 = xv[:, :, :, 0, :]
            x2 = xv[:, :, :, 1, :]
            bc = lambda ci, gi: tabs[:, ci, :, gi:gi + 1, :].broadcast_to([128, 2, 24, 32])
            tk = [K("tabs", 0), K("tabs", 1), K("qn")] + qkeys
            P.dve(M("tensor_tensor", out=ra[:], in0=x1, in1=bc(0, 0), op=ALU.mult), reads=tk, writes=[K("ra")])
            P.pool(M("tensor_tensor", out=rb[:], in0=x2, in1=bc(1, 1), op=ALU.mult), reads=tk, writes=[K("rb")])
            P.pool(M("tensor_tensor", out=ra2[:], in0=x2, in1=bc(0, 1), op=ALU.mult), reads=tk, writes=[K("ra2")])
            P.dve(M("tensor_tensor", out=rb2[:], in0=x1, in1=bc(1, 0), op=ALU.mult), reads=tk, writes=[K("rb2")])
            P.dve(M("tensor_tensor", out=kv_[:, :, :, 0, :], in0=ra[:], in1=rb[:], op=ALU.subtract),
                  reads=[K("ra"), K("rb")], writes=[K("kr", s, 0)])
            P.pool(M("tensor_tensor", out=kv_[:, :, :, 1, :], in0=ra2[:], in1=rb2[:], op=ALU.add),
                   reads=[K("ra2"), K("rb2")], writes=[K("kr", s, 1)])
            P.act(M("copy", out=qkb[:], in_=kr[s][:]), reads=[K("kr", s, 0), K("kr", s, 1)], writes=[K("qkb")])
            for grp in range(3):
                pp = grp % 2
                for i in range(8):
                    f = grp * 8 + i
                    P.pe(M("transpose", out=pT2[pp][:, i, :], in_=qkb[:, f * 128:(f + 1) * 128], identity=C.idb[:]),
                         reads=[K("qkb"), "idb"], writes=[K("pT2", pp)])
                P.act(M("copy", out=qkTs[s][:, grp * 8:(grp + 1) * 8, :], in_=pT2[pp][:]),
                      reads=[K("pT2", pp)], writes=[K("qkTs", s, grp)])
            P.dma("a1qk%d" % s, D.QKT[:, :, t * 128:(t + 1) * 128].rearrange("f p t -> p f t"), qkTs[s][:],
                  reads=[K("qkTs", s, g_) for g_ in range(3)], writes=[("dram", "QKT", t)])
            P.dma("a1v%d" % s, D.Vs[t * 128:(t + 1) * 128, :], vb[s][:], reads=[K("vb", s, j) for j in range(3)],
                  writes=[("dram", "Vs", t)])
            krk = [K("kr", s, 0), K("kr", s, 1)]
            vfk = [K("vf", s, j) for j in range(3)]
            for g, (win, dil) in enumerate(GROUPS):
                if t < 32:
                    r0 = t * 128 - (4096 - win)
                    if r0 < 0:
                        continue
                    P.dma("a1ok%d" % s, D.kvp[g][r0:r0 + 128, 0, :], kr[s][:, 1536 + g * 512:1536 + (g + 1) * 512], reads=krk)
                    P.dma("a1ov%d" % s, D.kvp[g][r0:r0 + 128, 1, :], vf[s][:, g * 512:(g + 1) * 512], reads=vfk)
                else:
                    for b in range(4):
                        P.dma("a1ok%d" % s, D.kvs[g][b, win - 8:win, 0, :], kr[s][b * 8:(b + 1) * 8, 1536 + g * 512:1536 + (g + 1) * 512], reads=krk)
                        P.dma("a1ov%d" % s, D.kvs[g][b, win - 8:win, 1, :], vf[s][b * 8:(b + 1) * 8, g * 512:(g + 1) * 512], reads=vfk)
        for g, (win, dil) in enumerate(GROUPS):
            for b in range(4):
                P.dma("a1cc%d" % (b % 2), D.kvs[g][b, 0:win - 8, :, :], D.cache[g][b, 8:win, :, :])


def attn_a2(P, C, nc, D):
    tag = "a2"
    K = lambda *a: (tag,) + a
    with contextlib.ExitStack() as st:
        sb = lambda name, shape, dt: st.enter_context(nc.sbuf_tensor(tag + name, shape, dt))
        ps = lambda name, shape, dt: st.enter_context(nc.psum_tensor(tag + name, shape, dt))
        OT = sb("OT", [64, 8, TOK], BF16)
        mask = sb("mask", [128, 512], BF16)
        ones32 = sb("ones32", [65, 64], F32)
        wo = sb("wo", [64, 8, DM], BF16)
        with contextlib.ExitStack() as st_s:
            sample_attn(P, C, nc, D, st_s, OT, K)
        P.barrier(C.dummy[:])
        st2 = st.enter_context(contextlib.ExitStack())
        sb2 = lambda name, shape, dt: st2.enter_context(nc.sbuf_tensor(tag + name, shape, dt))
        ps2 = lambda name, shape, dt: st2.enter_context(nc.psum_tensor(tag + name, shape, dt))
        qT = [sb2("qT%d" % i, [64, 4096], BF16) for i in range(2)]
        kT = [sb2("kT%d" % i, [64, 4096], BF16) for i in range(2)]
        Va = [sb2("Va%d" % i, [128, 32, 65], BF16) for i in range(2)]
        Oacc = sb2("Oacc", [65, 4096], F32)
        Pb = [sb2("Pb%d" % i, [128, 512], BF16) for i in range(4)]
        pS = [ps2("pS%d" % i, [128, 512], F32) for i in range(3)]
        pO = [ps2("pO%d" % i, [128, 512], F32) for i in range(4)]
        pcount = [0]
        pB = ps2("pB", [128, 512], F32)

        P.dma("a2m", mask[:], D.mask_p, writes=[K("mask")], eng="pool")
        P.pool(M("memset", ones32[:], 1.0), writes=[K("ones32")])
        for i in range(2):
            P.pool(M("memset", Va[i][:, :, 64:65], 1.0), writes=[K("Vone", i)])
        P.dma("a2wo", wo[:], D.w_o.rearrange("(h d) n -> d h n", d=64), writes=[K("wo")], eng="pool")

        for h in range(8):
            for g, (win, dil) in enumerate(GROUPS):
                u = h * 3 + g
                s2 = u % 2
                nkb = 32 // dil
                fq, fk, r0 = g * 4 + h // 2, 12 + g * 4 + h // 2, (h % 2) * 64
                qbuf = qT[s2]
                P.dma("a2q%d" % s2, qbuf[:], D.QKT[fq, r0:r0 + 64, 0:4096], reads=[("dram", "QKT", t) for t in range(32)], writes=[K("qT", s2)])
                P.dma("a2k%d" % s2, kT[s2][:], D.QKT[fk, r0:r0 + 64, 0:4096], reads=[("dram", "QKT", t) for t in range(32)], writes=[K("kT", s2)])
                vview = D.Vs[0:4096, :].rearrange("(kb n r) f -> n r kb f", n=128, r=dil)
                c0 = g * 512 + h * 64
                P.dma("a2v%d" % s2, Va[s2][:, :, 0:64].rearrange("p (r k) d -> p r k d", r=dil), vview[:, :, :, c0:c0 + 64],
                      reads=[("dram", "Vs", t) for t in range(32)], writes=[K("Va", s2, b_) for b_ in range(32)])
                npair = nkb // 2
                vak = [K("Va", s2, b_) for b_ in range(32)] + [K("Vone", s2)]
                for r in range(dil):
                    for kp in range(npair):
                        PB = pcount[0]
                        pcount[0] += 1
                        pp, pbi, pprev = PB % 3, PB % 4, (PB - 1) % 4
                        width = 0
                        for j in range(2):
                            kb = 2 * kp + j
                            nq = 256 if kb < nkb - 1 else 128
                            koff = dil * 128 * kb + r
                            kcols = kT[s2][:, koff:koff + dil * 127 + 1:dil]
                            qcols = qbuf[:, koff:koff + dil * (nq - 1) + 1:dil]
                            P.pe(M("matmul", pS[pp][:, j * 256:j * 256 + nq], lhsT=kcols, rhs=qcols, start=True, stop=True),
                                 reads=[K("kT", s2), K("qT", s2)], writes=[K("pS", pp)])
                            width = j * 256 + nq
                        P.act(M("activation", out=Pb[pbi][:, 0:width], in_=pS[pp][:, 0:width], func=AF.Exp, scale=0.125),
                              reads=[K("pS", pp)], writes=[K("Pb", pbi)])
                        meng = P.pool if PB % 3 == 2 else P.dve
                        meng(M("tensor_tensor", out=Pb[pbi][:, 0:width], in0=Pb[pbi][:, 0:width], in1=mask[:, 0:width], op=ALU.mult),
                             reads=[K("Pb", pbi), K("mask")], writes=[K("Pb", pbi)])
                        for j in range(2):
                            kb = 2 * kp + j
                            koff = dil * 128 * kb + r
                            po = (2 * PB + j) % 4
                            B = r * nkb + kb
                            if kb > 0:
                                prev = Pb[pbi][:, 128:256] if j == 1 else Pb[pprev][:, 384:512]
                                pk = K("Pb", pbi) if j == 1 else K("Pb", pprev)
                                P.pe(M("matmul", pO[po][0:65, 0:128], lhsT=Va[s2][:, B - 1, :], rhs=prev, start=True, stop=False),
                                     reads=vak + [pk], writes=[K("pO", po)])
                            P.pe(M("matmul", pO[po][0:65, 0:128], lhsT=Va[s2][:, B, :], rhs=Pb[pbi][:, j * 256:j * 256 + 128], start=(kb == 0), stop=True),
                                 reads=vak + [K("Pb", pbi)], writes=[K("pO", po)])
                            ocols = Oacc[:, koff:koff + dil * 127 + 1:dil]
                            blks = sorted(set((koff + dil * m) // 128 for m in (0, 127)))
                            blks = list(range(blks[0], blks[-1] + 1))
                            okeys = [K("Oacc", b_) for b_ in blks]
                            if g == 0:
                                P.act(M("copy", out=ocols, in_=pO[po][0:65, 0:128]), reads=[K("pO", po)], writes=okeys)
                            else:
                                P.dve(M("tensor_tensor", out=ocols, in0=ocols, in1=pO[po][0:65, 0:128], op=ALU.add),
                                      reads=[K("pO", po)] + okeys, writes=okeys)
            allk = [K("Oacc", b_) for b_ in range(32)]
            P.dve(M("reciprocal", out=Oacc[64:65, :], in_=Oacc[64:65, :]), reads=allk, writes=[K("rl")])
            for cb in range(8):
                P.pe(M("matmul", pB[0:64, :], lhsT=ones32[64:65, :], rhs=Oacc[64:65, cb * 512:(cb + 1) * 512], start=True, stop=True),
                     reads=[K("rl"), K("ones32")], writes=[K("pB")])
                P.dve(M("tensor_tensor", out=OT[:, h, cb * 512:(cb + 1) * 512], in0=Oacc[0:64, cb * 512:(cb + 1) * 512], in1=pB[0:64, :], op=ALU.mult),
                      reads=[K("pB")] + allk, writes=[K("OT", h, cb)] + [K("Oacc", b_) for b_ in range(cb * 4, cb * 4 + 4)])

        if getattr(D, "dbg_OT", None) is not None:
            P.dma("dbgot", D.dbg_OT, OT[:], reads=[K("OT", h_, c_) for h_ in range(8) for c_ in range(8)] + [K("OTs")], eng="pool", max_dma_last_dim=2048)
        st2.close()
        P.barrier(C.dummy[:])
        xt = [sb("xt%d" % i, [128, DM], F32) for i in range(2)]
        pW = [ps("pW%d" % i, [128, 512], F32) for i in range(2)]
        for t in range(NT):
            s = t % 2
            if t == 32:
                P.pool(M("memset", xt[s][:], 0.0), writes=[K("xt", s)])
                P.dma("a2x%d" % s, xt[s][0:32, :], D.xs[:, :], writes=[K("xt", s)])
                otk = [K("OTs")]
            else:
                P.dma("a2x%d" % s, xt[s][:], D.xp[t * 128:(t + 1) * 128, :], writes=[K("xt", s)])
                otk = [K("OT", h, t // 4) for h in range(8)]
            for half in range(2):
                for h in range(8):
                    P.pe(M("matmul", pW[half][:], lhsT=OT[:, h, t * 128:(t + 1) * 128], rhs=wo[:, h, half * 512:(half + 1) * 512],
                                                                start=(h == 0), stop=(h == 7)),
                         reads=otk + [K("wo")], writes=[K("pW", half)])
                P.dve(M("tensor_tensor", out=xt[s][:, half * 512:(half + 1) * 512], in0=xt[s][:, half * 512:(half + 1) * 512],
                                                               in1=pW[half][:], op=ALU.add),
                      reads=[K("xt", s), K("pW", half)], writes=[K("xt", s)])
            P.dma("a2st%d" % s, D.X1[t * 128:(t + 1) * 128, :], xt[s][:], reads=[K("xt", s)], writes=[("dram", "X1", t)])


def sample_attn(P, C, nc, D, st, OT, K):
    tag = "sa"
    sb = lambda name, shape, dt: st.enter_context(nc.sbuf_tensor(tag + name, shape, dt))
    ps = lambda name, shape, dt: st.enter_context(nc.psum_tensor(tag + name, shape, dt))
    qs = sb("qs", [128, 12, 32], BF16)
    ks = sb("ks", [128, 12, 32], BF16)
    vnew = [sb("vnew%d" % b, [8, 24, 65], BF16) for b in range(4)]
    kcf = sb("kcf", [128, 13, 512], F32)
    vcf = sb("vcf", [128, 13, 512], F32)
    kcb = sb("kcb", [128, 13, 512], BF16)
    kcT = sb("kcT", [128, 13, 4, 128], BF16)
    vca = sb("vca", [128, 13, 8, 65], BF16)
    msk = sb("msk", [128, 13, 8], BF16)
    mskn = sb("mskn", [8, 3, 8], BF16)
    Ps = sb("Ps", [128, 13, 8], BF16)
    Pn = sb("Pn", [8, 3, 8], BF16)
    osb = sb("osb", [8, 8, 64], BF16)
    rl = sb("rl", [8, 2], F32)
    pT = ps("pT", [128, 8, 128], BF16)
    pS = ps("pS", [128, 512], F32)[:, 0:104].rearrange("p (k q) -> p k q", q=8)
    pN = ps("pN", [128, 512], F32)[0:8, 0:24].rearrange("p (k q) -> p k q", q=8)
    pO = ps("pO", [128, 512], F32)[0:8, 0:65]
    pX = ps("pX", [128, 1024], BF16)[0:64, 0:64].rearrange("p (k q) -> p k q", q=8)
    SK = lambda *a: ("sa",) + a
    qk_r = [("dram", "QKT", 32)]
    P.dma("saq", qs[:], D.QKT[0:12, :, 4096:4128].rearrange("f p t -> p f t"), reads=qk_r, writes=[SK("qs")])
    P.dma("sak", ks[:], D.QKT[12:24, :, 4096:4128].rearrange("f p t -> p f t"), reads=qk_r, writes=[SK("ks")])
    P.dma("sam", msk[:], D.mask_s, writes=[SK("msk")], eng="pool")
    P.dma("samn", mskn[:], D.mask_n, writes=[SK("mskn")], eng="pool")
    P.pool(M("memset", vca[:, :, :, 64:65], 1.0), writes=[SK("vone")])
    for b in range(4):
        P.pool(M("memset", vnew[b][:, :, 64:65], 1.0), writes=[SK("vnone", b)])
        P.dma("savn%d" % b, vnew[b][:, :, 0:64], D.Vs[4096 + 8 * b:4096 + 8 * b + 8, :].rearrange("p (a d) -> p a d", d=64),
              reads=[("dram", "Vs", 32)], writes=[SK("vnew", b)])
    for b in range(4):
        for kv_i, dst in ((0, kcf), (1, vcf)):
            P.dma("sac%da" % kv_i, dst[:, 0, :], D.cache[0][b, :, kv_i, :], writes=[SK("cf", kv_i, 0)])
            P.dma("sac%db" % kv_i, dst[:, 1:5, :], D.cache[1][b, :, kv_i, :].rearrange("(k n) f -> n k f", n=128), writes=[SK("cf", kv_i, 1)])
            P.dma("sac%dc" % kv_i, dst[:, 5:13, :], D.cache[2][b, :, kv_i, :].rearrange("(n r) f -> n r f", r=16)[:, 0:8, :], writes=[SK("cf", kv_i, 2)])
        cfk = [SK("cf", 0, i) for i in range(3)]
        cfv = [SK("cf", 1, i) for i in range(3)]
        P.act(M("copy", out=kcb[:], in_=kcf[:]), reads=cfk, writes=[SK("kcb")])
        P.pool(M("tensor_copy", out=vca[:, :, :, 0:64], in_=vcf[:].rearrange("p k (h d) -> p k h d", d=64)), reads=cfv, writes=[SK("vca")])
        for grp in range(7):
            items = [(blk, hp) for blk in range(13) for hp in range(4)][grp * 8:(grp + 1) * 8]
            for i, (blk, hp) in enumerate(items):
                P.pe(M("transpose", out=pT[:, i, :], in_=kcb[:, blk, hp * 128:(hp + 1) * 128], identity=C.idb[:]),
                     reads=[SK("kcb"), "idb"], writes=[SK("pT")])
            n = len(items)
            b0, h0 = items[0]
            dstv = kcT[:].rearrange("p k h n -> p (k h) n")[:, grp * 8:grp * 8 + n, :]
            P.act(M("copy", out=dstv, in_=pT[:, 0:n, :]), reads=[SK("pT")], writes=[SK("kcT", grp)])
        kck = [SK("kcT", grp) for grp in range(7)]
        for h in range(8):
            pr = (h % 2) * 64
            hp = h // 2
            gof = lambda blk: 0 if blk == 0 else (1 if blk < 5 else 2)
            for blk in range(13):
                g = gof(blk)
                P.pe(M("matmul", pS[:, blk, :], lhsT=kcT[pr:pr + 64, blk, hp, :], rhs=qs[pr:pr + 64, g * 4 + hp, 8 * b:8 * b + 8],
                                                                         start=True, stop=True),
                     reads=kck + [SK("qs")], writes=[SK("pS")])
            for g in range(3):
                P.pe(M("matmul", pN[:, g, :], lhsT=ks[pr:pr + 64, g * 4 + hp, 8 * b:8 * b + 8], rhs=qs[pr:pr + 64, g * 4 + hp, 8 * b:8 * b + 8],
                                                              start=True, stop=True),
                     reads=[SK("ks"), SK("qs")], writes=[SK("pN")])
            P.act(M("activation", out=Ps[:], in_=pS[:], func=AF.Exp, scale=0.125), reads=[SK("pS")], writes=[SK("Ps")])
            P.act(M("activation", out=Pn[:], in_=pN[:], func=AF.Exp, scale=0.125), reads=[SK("pN")], writes=[SK("Pn")])
            P.dve(M("tensor_tensor", out=Ps[:], in0=Ps[:], in1=msk[:], op=ALU.mult), reads=[SK("Ps"), SK("msk")], writes=[SK("Ps")])
            P.dve(M("tensor_tensor", out=Pn[:], in0=Pn[:], in1=mskn[:], op=ALU.mult), reads=[SK("Pn"), SK("mskn")], writes=[SK("Pn")])
            for blk in range(13):
                P.pe(M("matmul", pO[:], lhsT=Ps[:, blk, :], rhs=vca[:, blk, h, :], start=(blk == 0), stop=False),
                     reads=[SK("Ps"), SK("vca"), SK("vone")], writes=[SK("pO")])
            for g in range(3):
                P.pe(M("matmul", pO[:], lhsT=Pn[:, g, :], rhs=vnew[b][:, g * 8 + h, :], start=False, stop=(g == 2)),
                     reads=[SK("Pn"), SK("vnew", b), SK("vnone", b)], writes=[SK("pO")])
            P.dve(M("reciprocal", out=rl[:, 0:1], in_=pO[:, 64:65]), reads=[SK("pO")], writes=[SK("rl")])
            P.dve(M("tensor_scalar", out=osb[:, h, :], in0=pO[:, 0:64], scalar1=rl[:, 0:1], scalar2=None, op0=ALU.mult),
                  reads=[SK("pO"), SK("rl")], writes=[SK("osb", h)])
        for h in range(8):
            P.pe(M("transpose", out=pX[:, h, :], in_=osb[:, h, :], identity=C.idb[0:8, 0:8]),
                 reads=[SK("osb", h_) for h_ in range(8)] + ["idb"], writes=[SK("pX")])
        P.act(M("copy", out=OT[:, :, 4096 + 8 * b:4096 + 8 * b + 8], in_=pX[:]), reads=[SK("pX")], writes=[K("OTs")])
    P.pool(M("memset", OT[:, :, 4128:TOK], 0.0), writes=[K("OTs")])


SEG_T = 4
NCH = 64
TWO_PI = 6.283185307179586
PI = 3.141592653589793


def ssm_phase(P, C, nc, D):
    tag = "ss"
    K = lambda *a: (tag,) + a
    with contextlib.ExitStack() as st:
        sb = lambda name, shape, dt: st.enter_context(nc.sbuf_tensor(tag + name, shape, dt))
        ps = lambda name, shape, dt: st.enter_context(nc.psum_tensor(tag + name, shape, dt))
        Wp2 = sb("Wp2", [128, 8, 8, 2, 2, 64], BF16)
        Vp = sb("Vp", [128, 32, 9, 2, 2, 16], BF16)
        Kblk = sb("Kblk", [128, 8, 8, 128], BF16)
        Aa = sb("Aa", [128, 2, 32], F32)
        Ab = sb("Ab", [128, 2, 32], F32)
        cnt = [0]

        def dv(name, *a, reads=(), writes=(), eng="dve", **k):
            return P.add(eng, M(name, *a, **k), reads, writes)

        with contextlib.ExitStack() as stp:
            sbp = lambda name, shape, dt: stp.enter_context(nc.sbuf_tensor(tag + "p" + name, shape, dt))
            psp = lambda name, shape, dt: stp.enter_context(nc.psum_tensor(tag + "p" + name, shape, dt))

            def lam_E(sfx, are_d, aim_d, ldt_d, shape, sbp):
                k_ = lambda n: K(sfx, n)
                t = {}
                for n in ("are", "aim", "ldt", "dt", "x1", "mag", "ang", "kf", "y", "m", "y2", "sin", "cos", "Lr", "Li",
                          "den", "t1", "t2", "nr", "Er", "Ei"):
                    t[n] = sbp(sfx + n, shape, F32)
                ki = sbp(sfx + "ki", shape, mybir.dt.int32)
                P.dma("ssp" + sfx + "a", t["are"][:], are_d, writes=[k_("are")])
                P.dma("ssp" + sfx + "b", t["aim"][:], aim_d, writes=[k_("aim")])
                P.dma("ssp" + sfx + "c", t["ldt"][:], ldt_d, writes=[k_("ldt")])
                P.act(M("activation", out=t["dt"][:], in_=t["ldt"][:], func=AF.Exp), reads=[k_("ldt")], writes=[k_("dt")])
                tt = lambda o, a, b, op: dv("tensor_tensor", out=t[o][:], in0=t[a][:], in1=t[b][:], op=op, reads=[k_(a), k_(b)], writes=[k_(o)])
                tt("x1", "are", "dt", ALU.mult)
                P.act(M("activation", out=t["mag"][:], in_=t["x1"][:], func=AF.Exp), reads=[k_("x1")], writes=[k_("mag")])
                tt("ang", "aim", "dt", ALU.mult)
                dv("tensor_scalar", out=t["kf"][:], in0=t["ang"][:], scalar1=1.0 / TWO_PI, scalar2=None, op0=ALU.mult, reads=[k_("ang")], writes=[k_("kf")])
                dv("tensor_copy", out=ki[:], in_=t["kf"][:], reads=[k_("kf")], writes=[k_("ki")])
                dv("tensor_copy", out=t["kf"][:], in_=ki[:], reads=[k_("ki")], writes=[k_("kf")])
                dv("scalar_tensor_tensor", out=t["y"][:], in0=t["kf"][:], scalar=-TWO_PI, in1=t["ang"][:], op0=ALU.mult, op1=ALU.add,
                   reads=[k_("kf"), k_("ang")], writes=[k_("y")])

                def wrap(yk):
                    dv("tensor_single_scalar", out=t["m"][:], in_=t[yk][:], scalar=PI, op=ALU.is_gt, reads=[k_(yk)], writes=[k_("m")])
                    dv("scalar_tensor_tensor", out=t[yk][:], in0=t["m"][:], scalar=-TWO_PI, in1=t[yk][:], op0=ALU.mult, op1=ALU.add,
                       reads=[k_("m"), k_(yk)], writes=[k_(yk)])
                    dv("tensor_single_scalar", out=t["m"][:], in_=t[yk][:], scalar=-PI, op=ALU.is_lt, reads=[k_(yk)], writes=[k_("m")])
                    dv("scalar_tensor_tensor", out=t[yk][:], in0=t["m"][:], scalar=TWO_PI, in1=t[yk][:], op0=ALU.mult, op1=ALU.add,
                       reads=[k_("m"), k_(yk)], writes=[k_(yk)])
                wrap("y")
                dv("tensor_scalar", out=t["y2"][:], in0=t["y"][:], scalar1=PI / 2, scalar2=None, op0=ALU.add, reads=[k_("y")], writes=[k_("y2")])
                wrap("y2")
                P.act(M("activation", out=t["sin"][:], in_=t["y"][:], func=AF.Sin), reads=[k_("y")], writes=[k_("sin")])
                P.act(M("activation", out=t["cos"][:], in_=t["y2"][:], func=AF.Sin), reads=[k_("y2")], writes=[k_("cos")])
                tt("Lr", "mag", "cos", ALU.mult)
                tt("Li", "mag", "sin", ALU.mult)
                tt("t1", "are", "are", ALU.mult)
                tt("t2", "aim", "aim", ALU.mult)
                tt("den", "t1", "t2", ALU.add)
                dv("reciprocal", out=t["den"][:], in_=t["den"][:], reads=[k_("den")], writes=[k_("den")])
                dv("tensor_scalar", out=t["nr"][:], in0=t["Lr"][:], scalar1=-1.0, scalar2=None, op0=ALU.add, reads=[k_("Lr")], writes=[k_("nr")])
                tt("t1", "nr", "are", ALU.mult)
                tt("t2", "Li", "aim", ALU.mult)
                tt("Er", "t1", "t2", ALU.add)
                tt("Er", "Er", "den", ALU.mult)
                tt("t1", "Li", "are", ALU.mult)
                tt("t2", "nr", "aim", ALU.mult)
                tt("Ei", "t1", "t2", ALU.subtract)
                tt("Ei", "Ei", "den", ALU.mult)
                return t, k_

            def cmul(o_r, o_i, a_r, a_i, b_r, b_i, tmp1, tmp2):
                def tt(o, a, b, op):
                    dv("tensor_tensor", out=o[0], in0=a[0], in1=b[0], op=op, reads=[a[1], b[1]], writes=[o[1]])
                tt(tmp1, a_r, b_r, ALU.mult)
                tt(tmp2, a_i, b_i, ALU.mult)
                tt(o_r, tmp1, tmp2, ALU.subtract)
                tt(tmp1, a_r, b_i, ALU.mult)
                tt(tmp2, a_i, b_r, ALU.mult)
                tt(o_i, tmp1, tmp2, ALU.add)

            tB, kB = lam_E("B", D.s_are_B, D.s_aim_B, D.s_ldt_B, [128, 512], sbp)
            bB = sbp("bB", [128, 2, 512], F32)
            P.dma("sspbB0", bB[:, 0, :], D.s_bre_B, writes=[K("bB", 0)])
            P.dma("sspbB1", bB[:, 1, :], D.s_bim_B, writes=[K("bB", 1)])
            Wall = sbp("Wall", [128, 8, 2, 512], F32)
            tm1 = sbp("tm1", [128, 512], F32)
            tm2 = sbp("tm2", [128, 512], F32)
            pm = sbp("pm", [128, 2], F32)
            P.dma("ssppm", pm[:], D.s_pm, writes=[K("pm")])
            T1, T2 = (tm1[:], K("tm1")), (tm2[:], K("tm2"))
            cmul((Wall[:, 0, 0, :], K("W", 0, 0)), (Wall[:, 0, 1, :], K("W", 0, 1)), (tB["Er"][:], kB("Er")), (tB["Ei"][:], kB("Ei")),
                 (bB[:, 0, :], K("bB", 0)), (bB[:, 1, :], K("bB", 1)), T1, T2)
            for k in range(1, 8):
                cmul((Wall[:, k, 0, :], K("W", k, 0)), (Wall[:, k, 1, :], K("W", k, 1)), (tB["Lr"][:], kB("Lr")), (tB["Li"][:], kB("Li")),
                     (Wall[:, k - 1, 0, :], K("W", k - 1, 0)), (Wall[:, k - 1, 1, :], K("W", k - 1, 1)), T1, T2)
            for k in range(8):
                for ri in range(2):
                    for par in range(2):
                        if (k + ri + par) % 2:
                            P.act(M("activation", out=Wp2[:, :, k, ri, par, :], in_=Wall[:, k, ri, :].rearrange("p (f q) -> p f q", q=64),
                                    func=AF.Copy, scale=pm[:, par:par + 1]), reads=[K("W", k, ri), K("pm")], writes=[K("Wp2")])
                        else:
                            dv("tensor_scalar", out=Wp2[:, :, k, ri, par, :], in0=Wall[:, k, ri, :].rearrange("p (f q) -> p f q", q=64),
                               scalar1=pm[:, par:par + 1], scalar2=None, op0=ALU.mult,
                               reads=[K("W", k, ri), K("pm")], writes=[K("Wp2")])
        P.barrier(C.dummy[:])
        with contextlib.ExitStack() as stp:
            sbp = lambda name, shape, dt: stp.enter_context(nc.sbuf_tensor(tag + "q" + name, shape, dt))
            psp = lambda name, shape, dt: stp.enter_context(nc.psum_tensor(tag + "q" + name, shape, dt))
            tC, kC = lam_E("C", D.s_are_C, D.s_aim_C, D.s_ldt_C, [128, 32], sbp)
            bC = sbp("bC", [128, 2, 32, 16], F32)
            P.dma("sspbC0", bC[:, 0], D.s_bre_C, writes=[K("bC", 0)])
            P.dma("sspbC1", bC[:, 1], D.s_bim_C, writes=[K("bC", 1)])
            Vall = sbp("Vall", [128, 9, 2, 32, 16], F32)
            P.dma("sspcC0", Vall[:, 0, 0], D.s_cre_C, writes=[K("V", 0, 0)])
            P.dma("sspcC1", Vall[:, 0, 1], D.s_cim_C, writes=[K("V", 0, 1)])
            Bb = sbp("Bb", [128, 2, 32, 16], F32)
            tc1 = sbp("tc1", [128, 32, 16], F32)
            tc2 = sbp("tc2", [128, 32, 16], F32)
            qm = sbp("qm", [128, 2], F32)
            P.dma("sspqm", qm[:], D.s_qm, writes=[K("qm")])
            U1, U2 = (tc1[:], K("tc1")), (tc2[:], K("tc2"))
            bc16 = lambda ap: ap.unsqueeze(2).broadcast_to([128, 32, 16])
            cmul((Bb[:, 0], K("Bb", 0)), (Bb[:, 1], K("Bb", 1)), (bC[:, 0], K("bC", 0)), (bC[:, 1], K("bC", 1)),
                 (bc16(tC["Er"][:]), kC("Er")), (bc16(tC["Ei"][:]), kC("Ei")), U1, U2)
            for k in range(1, 9):
                cmul((Vall[:, k, 0], K("V", k, 0)), (Vall[:, k, 1], K("V", k, 1)), (Vall[:, k - 1, 0], K("V", k - 1, 0)), (Vall[:, k - 1, 1], K("V", k - 1, 1)),
                     (bc16(tC["Lr"][:]), kC("Lr")), (bc16(tC["Li"][:]), kC("Li")), U1, U2)
            Bp = sbp("Bp", [128, 32, 2, 2, 16], BF16)
            for ri in range(2):
                for par in range(2):
                    dv("tensor_scalar", out=Bp[:, :, ri, par, :], in0=Bb[:, ri], scalar1=qm[:, par:par + 1], scalar2=None, op0=ALU.mult,
                       reads=[K("Bb", ri), K("qm")], writes=[K("Bp")])
                    for k in range(9):
                        dv("tensor_scalar", out=Vp[:, :, k, ri, par, :], in0=Vall[:, k, ri], scalar1=qm[:, par:par + 1],
                           scalar2=(1.0 if ri == 0 else -1.0), op0=ALU.mult, op1=ALU.mult,
                           reads=[K("V", k, ri), K("qm")], writes=[K("Vp")], eng="dve")
            L2 = sbp("L2", [128, 3, 2, 32], F32)
            ta1 = sbp("ta1", [128, 32], F32)
            ta2 = sbp("ta2", [128, 32], F32)
            A1_, A2_ = (ta1[:], K("ta1")), (ta2[:], K("ta2"))
            prev = ((tC["Lr"][:], kC("Lr")), (tC["Li"][:], kC("Li")))
            for i in range(3):
                cur = ((L2[:, i, 0, :], K("L2", i, 0)), (L2[:, i, 1, :], K("L2", i, 1)))
                cmul(cur[0], cur[1], prev[0], prev[1], prev[0], prev[1], A1_, A2_)
                prev = cur
            a8k = [K("L2", 2, 0), K("L2", 2, 1)]
            dv("tensor_copy", out=Aa[:, 0, :], in_=L2[:, 2, 0, :], reads=a8k, writes=[K("Aa")])
            dv("tensor_copy", out=Aa[:, 1, :], in_=L2[:, 2, 0, :], reads=a8k, writes=[K("Aa")])
            dv("tensor_scalar", out=Ab[:, 0, :], in0=L2[:, 2, 1, :], scalar1=-1.0, scalar2=None, op0=ALU.mult, reads=a8k, writes=[K("Ab")])
            dv("tensor_copy", out=Ab[:, 1, :], in_=L2[:, 2, 1, :], reads=a8k, writes=[K("Ab")])
            Kf = sbp("Kf", [128, 8, 128], F32)
            idf = sbp("idf", [128, 128], F32)
            dL = sbp("dL", [128, 8], F32)
            P.dma("sspid", idf[:], D.ident, writes=[K("idf")])
            P.dma("sspd", dL[:], D.s_d, writes=[K("dL")])
            pK = psp("pK", [128, 8, 32], F32)
            P.pool(M("memset", Kf[:], 0.0), writes=[K("Kf")])
            for fc in range(8):
                for q4 in range(4):
                    pr = 4 * fc + q4
                    for tau in range(8):
                        for ri in range(2):
                            P.pe(M("matmul", pK[32 * q4:32 * q4 + 32, tau, :], lhsT=Bp[:, pr, ri].rearrange("p a c -> p (a c)"),
                                   rhs=Vp[:, pr, tau, ri].rearrange("p a c -> p (a c)"), start=(ri == 0), stop=(ri == 1), tile_position=(0, 32 * q4)),
                                 reads=[K("Bp"), K("Vp")], writes=[K("pK")])
                    dv("tensor_copy", out=Kf[32 * q4:32 * q4 + 32, :, 32 * q4:32 * q4 + 32], in_=pK[32 * q4:32 * q4 + 32, :, :],
                       reads=[K("pK")], writes=[K("Kf")])
                dv("scalar_tensor_tensor", out=Kf[:, 0, :], in0=idf[:], scalar=dL[:, fc:fc + 1], in1=Kf[:, 0, :], op0=ALU.mult, op1=ALU.add,
                   reads=[K("Kf"), K("idf"), K("dL")], writes=[K("Kf")])
                P.act(M("copy", out=Kblk[:, fc], in_=Kf[:]), reads=[K("Kf")], writes=[K("Kblk", fc)])
        P.barrier(C.dummy[:])

        win = sb("win", [128, 8, DM], BF16)
        gbc = sb("gbc", [128, DM], F32)
        load_w_bf16(P, "ssw1", win, D.ssm_w_in.rearrange("(kc p) n -> p kc n", p=128), 8, K("win"))
        P.dma("ssg", gbc[:], D.norm_mix1.partition_broadcast(128), writes=[K("gbc")])
        xt = [sb("xt%d" % i, [128, DM], F32) for i in range(2)]
        junk = sb("junk", [128, DM], F32)
        ss = [sb("ss%d" % i, [128, 4], F32) for i in range(2)]
        hb = [sb("hb%d" % i, [128, DM], BF16) for i in range(2)]
        hT = [sb("hT%d" % i, [128, 8, 128], BF16) for i in range(2)]
        uT2 = [sb("uT%d" % i, [128, 8, SEG_T * 128], BF16) for i in range(2)]
        gT2 = [sb("gT0", [128, 8, SEG_T * 128], BF16)] * 2
        xs2 = [sb("xs%d" % i, [128, 2, 32, NCH + 1], F32) for i in range(2)]
        xhb2 = [sb("xhb%d" % i, [128, 2, 32, NCH], BF16) for i in range(2)]
        xs_s = sb("xs_s", [128, 2, 32, 2, 4], F32)
        xhb_s = sb("xhb_s", [128, 2, 32, 4], BF16)
        r1 = sb("r1", [128, 2, 32], F32)
        r2 = sb("r2", [128, 2, 32], F32)
        g1 = sb("g1", [128, 512], F32)
        g2 = sb("g2", [128, 512], F32)
        pT = ps("pT", [128, 8, 128], BF16)
        pU = [ps("pU%d" % i, [128, 4, 128], F32) for i in range(2)]
        pSs = [ps("pSs%d" % i, [128, 512], F32) for i in range(4)]
        pY = ps("pY", [128, 512], F32)
        P.pool(M("memset", xs2[0][:, :, :, 0:1], 0.0), writes=[K("xs", 0)])
        P.dma("ssst0", xs_s[:, :, :, 0, :], D.st0, writes=[K("xs_s", 0)])

        def segment(tiles, nch, sample, sgi):
            ncol = 128 * len(tiles)
            bi = sgi % 2
            uT, gT, xs, xhb = uT2[bi], gT2[bi], xs2[bi], xhb2[bi]
            K = lambda *a: (tag, bi) + a if a[0] in ("uT", "xs", "xhb") else (tag,) + a
            for i, t in enumerate(tiles):
                s = t % 2
                P.dma("ssx%d" % s, xt[s][:], D.X2[t * 128:(t + 1) * 128, :], reads=[("dram", "X2", t)], writes=[K("xt", s)])
                rms_to_hT(P, C, xt[s][:], K("xt", s), gbc[:], K("gbc"), hb[s], K("hb", s), hT[s][:], [K("hT", s)], pT, ss[s], K("ss", s), junk[:])
                for half in range(2):
                    for f4 in range(4):
                        fc = half * 4 + f4
                        for kc in range(8):
                            P.pe(M("matmul", pU[half][:, f4, :], lhsT=win[:, kc, fc * 128:(fc + 1) * 128], rhs=hT[s][:, kc, :], start=(kc == 0), stop=(kc == 7)),
                                 reads=[K("hT", s), K("win", kc)], writes=[K("pU", half)])
                    P.act(M("copy", out=uT[:, half * 4:half * 4 + 4, i * 128:(i + 1) * 128], in_=pU[half][:]), reads=[K("pU", half)], writes=[K("uT", i)])
            ukeys = [K("uT", i) for i in range(len(tiles))]
            for fc in range(8):
                for ri in range(2):
                    for tp in range(8):
                        for q4 in range(4):
                            P.pe(M("matmul", pSs[q4][:, ri * 64:ri * 64 + nch], lhsT=Wp2[32 * q4:32 * q4 + 32, fc, 7 - tp, ri].rearrange("p a c -> p (a c)"),
                                   rhs=uT[32 * q4:32 * q4 + 32, fc, tp:tp + 8 * (nch - 1) + 1:8], start=(tp == 0), stop=(tp == 7), tile_position=(32 * q4, 0)),
                                 reads=ukeys + [K("Wp2")], writes=[K("pSs", q4)])
                for q4 in range(4):
                    pr = 4 * fc + q4
                    src = pSs[q4][:, 0:128].rearrange("p (r j) -> p r j", r=2)[:, :, 0:nch]
                    if sample:
                        P.act(M("copy", out=xs_s[:, :, pr, 1, :], in_=src), reads=[K("pSs", q4)], writes=[K("xs_s", 1)])
                    else:
                        P.act(M("copy", out=xs[:, :, pr, 1:nch + 1], in_=src), reads=[K("pSs", q4)], writes=[K("xs")])
            if sample:
                X = xs_s[:, :, :, 0, :]
                S_ = xs_s[:, :, :, 1, :]
                bA = lambda a: a.unsqueeze(3).broadcast_to([128, 2, 32, 4])
                r1s = xhb_s
                dv("tensor_copy", out=xhb_s[:], in_=X, reads=[K("xs_s", 0)], writes=[K("xhb")])
                t1 = g1[:, 0:256].rearrange("p (r q b) -> p r q b", r=2, q=32)
                t2 = g2[:, 0:256].rearrange("p (r q b) -> p r q b", r=2, q=32)
                dv("tensor_tensor", out=t1, in0=X, in1=bA(Aa[:]), op=ALU.mult, reads=[K("xs_s", 0), K("Aa")], writes=[K("g1")])
                dv("tensor_tensor", out=t2[:, 0], in0=X[:, 1], in1=bA(Ab[:])[:, 0], op=ALU.mult, reads=[K("xs_s", 0), K("Ab")], writes=[K("g2")])
                dv("tensor_tensor", out=t2[:, 1], in0=X[:, 0], in1=bA(Ab[:])[:, 1], op=ALU.mult, reads=[K("xs_s", 0), K("Ab")], writes=[K("g2")])
                dv("tensor_tensor", out=t1, in0=t1, in1=t2, op=ALU.add, reads=[K("g1"), K("g2")], writes=[K("g1")])
                dv("tensor_tensor", out=S_, in0=S_, in1=t1, op=ALU.add, reads=[K("g1"), K("xs_s", 1)], writes=[K("xs_s", 1)])
                P.dma("ssosts", D.sts, xs_s[:, :, :, 1, :], reads=[K("xs_s", 1)])
                xin = xhb_s
            else:
                for j in range(nch):
                    X = xs[:, :, :, j]
                    S_ = xs[:, :, :, j + 1]
                    dv("tensor_tensor", out=r1[:], in0=X, in1=Aa[:], op=ALU.mult, reads=[K("xs"), K("Aa")], writes=[K("r1")])
                    dv("tensor_tensor", out=r2[:, 0], in0=X[:, 1], in1=Ab[:, 0], op=ALU.mult, reads=[K("xs"), K("Ab")], writes=[K("r2")])
                    dv("tensor_tensor", out=r2[:, 1], in0=X[:, 0], in1=Ab[:, 1], op=ALU.mult, reads=[K("xs"), K("Ab")], writes=[K("r2")])
                    dv("tensor_tensor", out=r1[:], in0=r1[:], in1=r2[:], op=ALU.add, reads=[K("r1"), K("r2")], writes=[K("r1")])
                    dv("tensor_tensor", out=S_, in0=S_, in1=r1[:], op=ALU.add, reads=[K("r1"), K("xs")], writes=[K("xs")])
                P.act(M("copy", out=xhb[:], in_=xs[:, :, :, 0:nch]), reads=[K("xs")], writes=[K("xhb")])
                xin = xhb
            for fc in range(8):
                for t in range(8):
                    o = pY[:, t * 64:t * 64 + nch]
                    for tp in range(t + 1):
                        P.pe(M("matmul", o, lhsT=Kblk[:, fc, t - tp, :], rhs=uT[:, fc, tp:tp + 8 * (nch - 1) + 1:8], start=(tp == 0), stop=False),
                             reads=ukeys + [K("Kblk", fc)], writes=[K("pY")])
                    for ri in range(2):
                        for q4 in range(4):
                            pr = 4 * fc + q4
                            P.pe(M("matmul", pY[32 * q4:32 * q4 + 32, t * 64:t * 64 + nch], lhsT=Vp[:, pr, t + 1, ri].rearrange("p a c -> p (a c)"),
                                   rhs=xin[:, ri, pr, 0:nch], start=False, stop=(q4 == 3 and ri == 1), tile_position=(0, 32 * q4)),
                                 reads=[K("Vp"), K("xhb")], writes=[K("pY")])
                yv = pY[:].rearrange("p (t j) -> p t j", t=8)[:, :, 0:nch]
                v3 = lambda ap: ap[:, 0:8 * nch].rearrange("p (t j) -> p t j", t=8)
                P.act(M("activation", out=v3(g1[:]), in_=yv, func=AF.Square), reads=[K("pY")], writes=[K("g1")])
                dv("tensor_scalar", out=v3(g1[:]), in0=v3(g1[:]), scalar1=0.044715, scalar2=1.0, op0=ALU.mult, op1=ALU.add, reads=[K("g1")], writes=[K("g1")])
                dv("tensor_tensor", out=v3(g2[:]), in0=v3(g1[:]), in1=yv, op=ALU.mult, reads=[K("g1"), K("pY")], writes=[K("g2")])
                P.act(M("activation", out=v3(g1[:]), in_=v3(g2[:]), func=AF.Sigmoid, scale=1.5957691216057308), reads=[K("g2")], writes=[K("g1")])
                gdst = gT[:, fc, 0:8 * nch].rearrange("p (j t) -> p t j", t=8)
                dv("tensor_tensor", out=gdst, in0=v3(g1[:]), in1=yv, op=ALU.mult, reads=[K("g1"), K("pY")], writes=[K("gT", fc)])
            gkeys = [K("gT", fc) for fc in range(8)]
            c0 = tiles[0] * 128
            P.dma("ssgst", D.GT[:, :, c0:c0 + ncol].rearrange("f p t -> p f t"), gT[:, :, 0:ncol], reads=gkeys,
                  writes=[("dram", "GT", t) for t in tiles])

        nseg = 32 // SEG_T
        for sg_ in range(nseg):
            segment(list(range(sg_ * SEG_T, (sg_ + 1) * SEG_T)), NCH, False, sg_)
            if sg_ < nseg - 1:
                P.dve(M("tensor_copy", out=xs2[(sg_ + 1) % 2][:, :, :, 0:1], in_=xs2[sg_ % 2][:, :, :, NCH:NCH + 1]),
                      reads=[(tag, sg_ % 2, "xs")], writes=[(tag, (sg_ + 1) % 2, "xs")])
        lastb = (nseg - 1) % 2
        P.dve(M("tensor_copy", out=r2[:], in_=xs2[lastb][:, :, :, NCH]), reads=[(tag, lastb, "xs")], writes=[K("r2")])
        P.dma("ssostp", D.stp, r2[:], reads=[K("r2")])
        segment([32], 4, True, nseg)
    P.barrier(C.dummy[:])
    glu_phase(P, C, nc, D)


def glu_phase(P, C, nc, D):
    tag = "gl"
    K = lambda *a: (tag,) + a
    with contextlib.ExitStack() as st:
        sb = lambda name, shape, dt: st.enter_context(nc.sbuf_tensor(tag + name, shape, dt))
        ps = lambda name, shape, dt: st.enter_context(nc.psum_tensor(tag + name, shape, dt))
        wglu = sb("wglu", [128, 8, 2 * DM], BF16)
        load_w_bf16(P, "glw", wglu, D.ssm_w_glu.rearrange("(kc p) n -> p kc n", p=128), 8, K("wglu"))
        gt = [sb("gt%d" % i, [128, 8, 128], BF16) for i in range(3)]
        xt = [sb("xt%d" % i, [128, DM], F32) for i in range(3)]
        sg = [sb("sg%d" % i, [128, 512], F32) for i in range(4)]
        pV = [ps("pV%d" % i, [128, 512], F32) for i in range(4)]
        pG = [ps("pG%d" % i, [128, 512], F32) for i in range(4)]
        for t in range(NT):
            s = t % 3
            P.dma("glg%d" % s, gt[s][:], D.GT[:, :, t * 128:(t + 1) * 128].rearrange("f p t -> p f t"), reads=[("dram", "GT", t)], writes=[K("gt", s)])
            P.dma("glx%d" % s, xt[s][:], D.X2[t * 128:(t + 1) * 128, :], reads=[("dram", "X2", t)], writes=[K("xt", s)])
            for half in range(2):
                q = (2 * t + half) % 4
                for vg, pp in ((0, pV), (1, pG)):
                    nb = vg * 2 + half
                    for fc in range(8):
                        P.pe(M("matmul", pp[q][:], lhsT=gt[s][:, fc, :], rhs=wglu[:, fc, nb * 512:(nb + 1) * 512], start=(fc == 0), stop=(fc == 7)),
                             reads=[K("gt", s), K("wglu", fc)], writes=[K("p", vg, q)])
                P.act(M("activation", out=sg[q][:], in_=pG[q][:], func=AF.Sigmoid), reads=[K("p", 1, q)], writes=[K("sg", q)])
                P.dve(M("tensor_tensor", out=sg[q][:], in0=sg[q][:], in1=pV[q][:], op=ALU.mult), reads=[K("sg", q), K("p", 0, q)], writes=[K("sg", q)])
                P.pool(M("tensor_tensor", out=xt[s][:, half * 512:(half + 1) * 512], in0=xt[s][:, half * 512:(half + 1) * 512], in1=sg[q][:], op=ALU.add),
                       reads=[K("sg", q), K("xt", s)], writes=[K("xt", s)])
            P.dma("glst%d" % s, D.X3[t * 128:(t + 1) * 128, :], xt[s][:], reads=[K("xt", s)], writes=[("dram", "X3", t)])


def build(stage=9):
    nc = bass.Bass("TRN2", target_bir_lowering=False)
    D = Ctx()
    C = Ctx()
    din = lambda name, shape, dt=F32: nc.dram_tensor(name, shape, dt, kind="ExternalInput").ap()
    dout = lambda name, shape, dt=F32: nc.dram_tensor(name, shape, dt, kind="ExternalOutput").ap()
    dscr = lambda name, shape, dt=F32: nc.dram_tensor(name, shape, dt, kind="Internal").ap()
    D.xp = din("xp", [4096, DM])
    D.xs = din("xs", [32, DM])
    D.rope = din("rope", [NT, 128, 2, 32])
    ident = din("ident", [128, 128])
    D.mask_p = din("mask_p", [128, 512])
    D.mask_s = din("mask_s", [128, 13, 8])
    D.mask_n = din("mask_n", [8, 3, 8])
    D.cache = [din("cache%d" % g, [4, win, 2, 512]) for g, (win, dil) in enumerate(GROUPS)]
    norm_mix = din("norm_mix", [2, DM])
    norm_ffn = din("norm_ffn", [2, DM])
    D.norm_mix0 = norm_mix[0]
    D.w_qkv = din("w_qkv", [DM, 4608])
    D.q_norm = din("q_norm", [2, 32])
    D.k_norm = din("k_norm", [2, 32])
    D.w_o = din("w_o", [512, DM])
    wg = din("ffn_w_gate", [2, DM, DFF])
    wu = din("ffn_w_up", [2, DM, DFF])
    wd = din("ffn_w_down", [2, DFF, DM])
    D.norm_mix1 = norm_mix[1]
    D.ident = ident
    D.ssm_w_in = din("ssm_w_in", [DM, DM])
    D.ssm_w_glu = din("ssm_w_glu", [DM, 2 * DM])
    for nm in ("are", "aim", "ldt", "bre", "bim"):
        setattr(D, "s_%s_B" % nm, din("s_%s_B" % nm, [128, 512]))
    for nm in ("are", "aim", "ldt"):
        setattr(D, "s_%s_C" % nm, din("s_%s_C" % nm, [128, 32]))
    for nm in ("bre", "bim", "cre", "cim"):
        setattr(D, "s_%s_C" % nm, din("s_%s_C" % nm, [128, 32, 16]))
    D.s_d = din("s_d", [128, 8])
    D.s_pm = din("s_pm", [128, 2])
    D.s_qm = din("s_qm", [128, 2])
    D.st0 = din("st0", [128, 2, 32, 4])
    D.stp = dout("stp", [128, 2, 32])
    D.sts = dout("sts", [128, 2, 32, 4])
    D.yp = dout("yp", [4096, DM])
    D.ys = dout("ys", [32, DM])
    D.kvp = [dout("kvp%d" % g, [win, 2, 512]) for g, (win, dil) in enumerate(GROUPS)]
    D.kvs = [dout("kvs%d" % g, [4, win, 2, 512]) for g, (win, dil) in enumerate(GROUPS)]
    D.dbg_OT = dout("dbg_OT", [64, 8, TOK]) if stage < 9 else None
    D.dbg_QKT = dout("dbg_QKT", [24, 128, TOK], BF16) if stage < 9 else None
    D.dbg_Vs = dout("dbg_Vs", [TOK, 1536], BF16) if stage < 9 else None
    D.QKT = dscr("QKT", [24, 128, TOK], BF16)
    D.Vs = dscr("Vs", [TOK, 1536], BF16)
    D.X1 = dscr("X1", [TOK, DM])
    D.X2 = dscr("X2", [TOK, DM])
    D.X3 = dscr("X3", [TOK, DM])
    D.GT = dscr("GT", [8, 128, TOK], BF16)
    P = Prog(nc)
    with contextlib.ExitStack() as st:
        C.idb = st.enter_context(nc.sbuf_tensor("idb", [128, 128], BF16))
        C.dummy = st.enter_context(nc.sbuf_tensor("bar_dummy", [128, 8], F32))
        P.dma("id", C.idb[:], ident, writes=["idb"], eng="pool")
        attn_a1(P, C, nc, D)
        if D.dbg_QKT is not None:
            for f in range(24):
                P.dma("dbgq", D.dbg_QKT[f], D.QKT[f], reads=[("dram", "QKT", t) for t in range(NT)])
            for t in range(NT):
                P.dma("dbgv", D.dbg_Vs[t * 128:(t + 1) * 128, :], D.Vs[t * 128:(t + 1) * 128, :], reads=[("dram", "Vs", t)])
        P.barrier(C.dummy[:])
        attn_a2(P, C, nc, D)
        P.barrier(C.dummy[:])
        if stage == 1:
            ffn_phase(P, C, nc, D.X1, "X1", None, None, wg[0], wu[0], wd[0], norm_ffn[0], "f0", final_out=(D.yp, D.ys))
        else:
            ffn_phase(P, C, nc, D.X1, "X1", D.X2, "X2", wg[0], wu[0], wd[0], norm_ffn[0], "f0")
            P.barrier(C.dummy[:])
            ssm_phase(P, C, nc, D)
            P.barrier(C.dummy[:])
            if stage == 2:
                for t in range(32):
                    P.dma("dbgx3", D.yp[t * 128:(t + 1) * 128, :], D.X3[t * 128:(t + 1) * 128, :], reads=[("dram", "X3", t)])
                P.dma("dbgx3", D.ys[:, :], D.X3[4096:4128, :], reads=[("dram", "X3", 32)])
            else:
                ffn_phase(P, C, nc, D.X3, "X3", None, None, wg[1], wu[1], wd[1], norm_ffn[1], "f1", final_out=(D.yp, D.ys))
        P.emit()
    return nc


def host_consts():
    half = 32
    inv = (10000.0 ** (-np.arange(half, dtype=np.float32) / half)).astype(np.float32)
    pos = np.zeros((NT, 128), np.float32)
    pos[:32] = np.arange(4096, dtype=np.float32).reshape(32, 128)
    pos[32] = 16384 + (np.arange(128) % 8)
    ang = pos[:, :, None] * inv[None, None, :]
    rope = np.stack([np.cos(ang), np.sin(ang)], axis=2).astype(np.float32)
    n = np.arange(128)[:, None]
    m = np.arange(128)[None, :]
    mask_p = np.concatenate([(n <= m), (n >= m), (n <= m), (n >= m)], axis=1).astype(np.float32)
    mask_s = np.zeros((128, 13, 8), np.float32)
    i = np.arange(8)[None, :]
    mask_s[:, 0, :] = (n >= i)
    for k in range(4):
        row = 128 * k + n
        mask_s[:, 1 + k, :] = (row % 4 == i % 4) & (row >= i)
    for r in range(8):
        mask_s[:, 5 + r, :] = (r == i)
    mask_n = np.zeros((8, 3, 8), np.float32)
    kn = np.arange(8)[:, None]
    for g, (win, dil) in enumerate(GROUPS):
        mask_n[:, g, :] = (kn <= i) & ((i - kn) % dil == 0)
    return dict(rope=rope, ident=np.eye(128, dtype=np.float32), mask_p=mask_p, mask_s=mask_s, mask_n=mask_n)


def ssm_layouts(inp, c):
    f = lambda a: np.ascontiguousarray(a, dtype=np.float32)
    a_re, a_im, ldt = inp["ssm_a_re"][0], inp["ssm_a_im"][0], inp["ssm_log_dt"][0]
    b_re, b_im = inp["ssm_b_re"][0], inp["ssm_b_im"][0]
    c_re, c_im = inp["ssm_c_re"][0], inp["ssm_c_im"][0]
    out = {}
    def lb_gp(a):
        x = a.reshape(8, 8, 64)
        x = np.broadcast_to(x[:, :, None, :], (8, 8, 16, 64))
        return f(x.transpose(1, 2, 0, 3).reshape(128, 512))
    out["s_are_B"] = lb_gp(a_re)
    out["s_aim_B"] = lb_gp(a_im)
    out["s_ldt_B"] = lb_gp(np.broadcast_to(ldt[:, None], (64, 64)))
    lb_b = lambda b: f(b.reshape(8, 8, 64, 16).transpose(1, 3, 0, 2).reshape(128, 512))
    out["s_bre_B"] = lb_b(b_re)
    out["s_bim_B"] = lb_b(b_im)
    lc_gp = lambda a: f(a.reshape(32, 2, 64).transpose(1, 2, 0).reshape(128, 32))
    out["s_are_C"] = lc_gp(a_re)
    out["s_aim_C"] = lc_gp(a_im)
    out["s_ldt_C"] = lc_gp(np.broadcast_to(ldt[:, None], (64, 64)))
    lc_b = lambda b: f(b.reshape(32, 2, 64, 16).transpose(1, 2, 0, 3).reshape(128, 32, 16))
    out["s_bre_C"] = lc_b(b_re)
    out["s_bim_C"] = lc_b(b_im)
    lc_c = lambda cc: f(cc.reshape(32, 2, 16, 64).transpose(1, 3, 0, 2).reshape(128, 32, 16))
    out["s_cre_C"] = lc_c(c_re)
    out["s_cim_C"] = lc_c(c_im)
    out["s_d"] = f(inp["ssm_d"][0].reshape(8, 128).T)
    g8 = np.arange(128) // 16
    out["s_pm"] = f(np.stack([(g8 % 2 == 0), (g8 % 2 == 1)], axis=1))
    par = np.arange(128) // 64
    out["s_qm"] = f(np.stack([(par == 0), (par == 1)], axis=1))
    st = inp["state_ssm"][0, 4 * c:4 * c + 4]
    out["st0"] = f(st.reshape(4, 32, 2, 64, 2).transpose(2, 3, 4, 1, 0).reshape(128, 2, 32, 4))
    return out


def make_in_maps(inp):
    cst = host_consts()
    maps = []
    for c in range(8):
        m = dict(cst)
        m["xp"] = np.ascontiguousarray(inp["x_prompt"][c])
        m["xs"] = np.ascontiguousarray(inp["x_sample"][4 * c:4 * c + 4].reshape(32, DM))
        for g, nm in enumerate(("cache_kv_w128", "cache_kv_w512", "cache_kv_w2048")):
            a = inp[nm][0, 4 * c:4 * c + 4]
            m["cache%d" % g] = np.ascontiguousarray(a.reshape(4, a.shape[1], 2, 512))
        m["norm_mix"] = inp["norm_mix"]
        m["norm_ffn"] = inp["norm_ffn"]
        m["w_qkv"] = inp["w_qkv"][0]
        m["q_norm"] = inp["q_norm"].reshape(2, 32)
        m["k_norm"] = inp["k_norm"].reshape(2, 32)
        m["w_o"] = inp["w_o"][0]
        m["ffn_w_gate"] = inp["ffn_w_gate"]
        m["ffn_w_up"] = inp["ffn_w_up"]
        m["ffn_w_down"] = inp["ffn_w_down"]
        m["ssm_w_in"] = inp["ssm_w_in"][0]
        m["ssm_w_glu"] = inp["ssm_w_glu"][0]
        m.update(ssm_layouts(inp, c))
        maps.append(m)
    return maps


_NC_CACHE = {}


def kernel(**inputs):
    inp = {k: np.asarray(v) for k, v in inputs.items()}
    if "nc" not in _NC_CACHE:
        _NC_CACHE["nc"] = build(stage=9)
    nc = _NC_CACHE["nc"]
    maps = make_in_maps(inp)
    res = run_bass_kernel_spmd(nc, maps, core_ids=list(range(8))).results
    y_p = np.stack([r["yp"] for r in res], axis=0)
    y_s = np.concatenate([r["ys"].reshape(4, 8, DM) for r in res], axis=0)
    kvp = [np.stack([r["kvp%d" % g].reshape(win, 2, 8, 64) for r in res], axis=0)[None] for g, (win, dil) in enumerate(GROUPS)]
    kvs = [np.concatenate([r["kvs%d" % g].reshape(4, win, 2, 8, 64) for r in res], axis=0)[None] for g, (win, dil) in enumerate(GROUPS)]
    stp = np.stack([r["stp"].reshape(2, 64, 2, 32).transpose(3, 0, 1, 2).reshape(64, 64, 2) for r in res], axis=0)[None]
    sts = np.concatenate([r["sts"].reshape(2, 64, 2, 32, 4).transpose(4, 3, 0, 1, 2).reshape(4, 64, 64, 2) for r in res], axis=0)[None]
    f = lambda a: np.ascontiguousarray(a, dtype=np.float32)
    return (f(y_p), f(y_s), f(kvp[0]), f(kvp[1]), f(kvp[2]), f(stp), f(kvs[0]), f(kvs[1]), f(kvs[2]), f(sts))
```
